# Optimizing a Trainium2 kernel written in Bass

```python
import math
import jax
import jax.numpy as jnp
from jax import lax
import numpy as np

D_MODEL = 1024
BATCH = 16
SEQ = 2048
DEPTH = 2
DEC_BATCH = 16
DEC_SEQ = 64
PAST_LEN = 2048

CHUNK = 64
EPS = 1e-6
NEG_INF = -1e30
A_HEADS = 4
A_DK = 128
A_DV = 128
A_QK = A_HEADS * A_DK
A_WIDTH = A_HEADS * A_DV
A_CONV = 4
B_HEADS = 4
B_DQK = 64
B_DV = 2 * B_DQK
B_QK = B_HEADS * 2 * B_DQK
B_WIDTH = B_HEADS * B_DV
Q_BLOCK = 128
C_WINDOWS = (2, 4, 8, 16)
C_GROUPS = 4
C_GROUP_DIM = 128
C_WIDTH = C_GROUPS * C_GROUP_DIM
C_POOL_BUF = 15
MIX_WIDTH = A_WIDTH + B_WIDTH + C_WIDTH
IN_WIDTHS = (A_QK, A_QK, A_WIDTH, A_WIDTH, A_HEADS, A_HEADS, B_QK, B_QK, B_WIDTH, C_WIDTH)
IN_WIDTH = 2 * A_QK + 2 * A_WIDTH + 2 * A_HEADS + 2 * B_QK + B_WIDTH + C_WIDTH
D_FF = 2816
FFN_CONV = 3

kernel_name = 'hybrid_stream_deltanet_diffattn_pool_step'


def rmsnorm(x, g):
    xf = x.astype(jnp.float32)
    y = xf * lax.rsqrt(jnp.mean(xf * xf, axis=-1, keepdims=True) + EPS)
    return (y * g.astype(jnp.float32)).astype(x.dtype)


def l2norm(x):
    xf = x.astype(jnp.float32)
    return xf * lax.rsqrt(jnp.sum(xf * xf, axis=-1, keepdims=True) + EPS)


def split_cols(t, widths):
    out, start = [], 0
    for w in widths:
        out.append(t[..., start:start + w])
        start += w
    return out


def causal_dwconv(x, buf, w, b):
    width = w.shape[0]
    length = x.shape[1]
    xp = jnp.concatenate([buf.astype(x.dtype), x], axis=1)
    y = b
    for i in range(width):
        y = y + xp[:, i:i + length] * w[i]
    return y, xp[:, xp.shape[1] - (width - 1):]


def gated_delta_rule(q, k, v, g, beta, s0):
    bsz, length, heads, dk = q.shape
    dv = v.shape[-1]
    n = -(-length // CHUNK)
    pad = n * CHUNK - length

    def prep(t):
        t = jnp.pad(t, [(0, 0), (0, pad)] + [(0, 0)] * (t.ndim - 2))
        t = t.reshape((bsz, n, CHUNK) + t.shape[2:])
        return jnp.moveaxis(t, 3, 1)

    q, k, v, g, beta = prep(q), prep(k), prep(v), prep(g), prep(beta)
    G = jnp.cumsum(g, axis=-1)
    idx = jnp.arange(CHUNK)
    incl = idx[:, None] >= idx[None, :]
    strict = idx[:, None] > idx[None, :]
    diff = G[..., :, None] - G[..., None, :]
    decay = jnp.where(incl, jnp.exp(jnp.where(incl, diff, 0.0)), 0.0)
    kk = jnp.einsum('bhncd,bhnsd->bhncs', k, k)
    lmat = jnp.where(strict, beta[..., :, None] * decay * kk, 0.0)
    eye = jnp.eye(CHUNK, dtype=jnp.float32)
    gam = jnp.exp(G)
    rhs = jnp.concatenate([v * beta[..., None], k * (beta * gam)[..., None]], axis=-1)
    uw = lax.linalg.triangular_solve(eye + lmat, rhs, left_side=True, lower=True, unit_diagonal=True)
    u_blk, w_blk = uw[..., :dv], uw[..., dv:]
    qk = jnp.einsum('bhncd,bhnsd->bhncs', q, k) * decay
    q_g = q * gam[..., None]
    k_tail = k * jnp.exp(G[..., -1:] - G)[..., None]
    g_last = gam[..., -1]

    def step(state, xs):
        u_c, w_c, qg_c, qk_c, kt_c, gl_c = xs
        delta = u_c - jnp.einsum('bhcd,bhde->bhce', w_c, state)
        o = jnp.einsum('bhcd,bhde->bhce', qg_c, state) + jnp.einsum('bhcs,bhse->bhce', qk_c, delta)
        state = state * gl_c[..., None, None] + jnp.einsum('bhcd,bhce->bhde', kt_c, delta)
        return state, o

    xs = tuple(jnp.moveaxis(t, 2, 0) for t in (u_blk, w_blk, q_g, qk, k_tail, g_last))
    s_final, o = lax.scan(step, s0, xs)
    o = jnp.moveaxis(jnp.moveaxis(o, 0, 2), 1, 3).reshape(bsz, n * CHUNK, heads, dv)
    return o[:, :length], s_final


def diff_softmax_weights(s, lam):
    p = jax.nn.softmax(s, axis=-1)
    return p[:, :, 0] - lam * p[:, :, 1]


def diff_attn_prompt(q, k, v, lam):
    bsz, length = q.shape[:2]
    nb = length // Q_BLOCK
    scale = B_DQK ** -0.5
    q_blocks = jnp.moveaxis(q.reshape(bsz, nb, Q_BLOCK, B_HEADS, 2, B_DQK), 1, 0)
    k_chunk = jnp.arange(length) // CHUNK

    def one_block(args):
        qi, bi = args
        q_chunk = (bi * Q_BLOCK + jnp.arange(Q_BLOCK)) // CHUNK
        mask = k_chunk[None, :] <= q_chunk[:, None]
        s = jnp.einsum('bqhmd,bkhmd->bhmqk', qi, k, preferred_element_type=jnp.float32) * scale
        s = jnp.where(mask, s, NEG_INF)
        w = diff_softmax_weights(s, lam)
        return jnp.einsum('bhqk,bkhe->bqhe', w.astype(v.dtype), v)

    o = lax.map(one_block, (q_blocks, jnp.arange(nb)))
    return jnp.moveaxis(o, 0, 1).reshape(bsz, length, B_HEADS, B_DV)


def diff_attn_sample(q, k_all, v_all, lam):
    s = jnp.einsum('bqhmd,bkhmd->bhmqk', q, k_all, preferred_element_type=jnp.float32) * (B_DQK ** -0.5)
    w = diff_softmax_weights(s, lam)
    return jnp.einsum('bhqk,bkhe->bqhe', w.astype(v_all.dtype), v_all)


def pool_mixer(xc, buf, pos0, c_w, c_scale):
    bsz, length, _ = xc.shape
    xp_raw = jnp.concatenate([buf.astype(xc.dtype), xc], axis=1)
    xp = xp_raw.astype(jnp.float32)
    cs = jnp.concatenate([jnp.zeros((bsz, 1, C_WIDTH), jnp.float32), jnp.cumsum(xp, axis=1)], axis=1)
    pos = pos0 + jnp.arange(length)
    hi = cs[:, C_POOL_BUF + 1:C_POOL_BUF + 1 + length]
    outs = []
    for gi, win in enumerate(C_WINDOWS):
        sl = slice(gi * C_GROUP_DIM, (gi + 1) * C_GROUP_DIM)
        lo = cs[:, C_POOL_BUF + 1 - win:C_POOL_BUF + 1 - win + length, sl]
        cnt = jnp.minimum(pos + 1, win).astype(jnp.float32)[None, :, None]
        pooled = (hi[..., sl] - lo) / cnt - xp[:, C_POOL_BUF:, sl]
        outs.append(jnp.einsum('blc,cd->bld', pooled, c_w[gi].astype(jnp.float32)))
    y = jnp.concatenate(outs, axis=-1) * c_scale.astype(jnp.float32)
    return y.astype(xc.dtype), xp_raw[:, xp_raw.shape[1] - C_POOL_BUF:]


def conv_ffn(h, buf, w_up, w_conv, b_conv, w_down):
    u = jnp.einsum('bld,df->blf', h, w_up)
    u, new_buf = causal_dwconv(u, buf, w_conv, b_conv)
    gate, val = split_cols(u, (D_FF, D_FF))
    return jnp.einsum('blf,fd->bld', jax.nn.silu(gate) * val, w_down), new_buf


def trunk_layer(x, pos0, delta_s0, conv_buf, past_k, past_v, pool_buf, ffn_buf,
                norm_mix, w_in, a_conv_w, a_conv_b, a_log, a_dt_bias, a_norm,
                b_lambda, b_norm, c_w, c_scale, w_out,
                norm_ffn, ffn_up, ffn_conv_w, ffn_conv_b, ffn_down, lam_init):
    bsz, length, _ = x.shape
    h = rmsnorm(x, norm_mix)
    proj = jnp.einsum('bld,de->ble', h, w_in)
    aq, ak, av, az, aa, ab, bq, bk, bv, cx = split_cols(proj, IN_WIDTHS)

    qkv, new_conv = causal_dwconv(jnp.concatenate([aq, ak, av], axis=-1), conv_buf, a_conv_w, a_conv_b)
    qkv = jax.nn.silu(qkv)
    q, k, v = split_cols(qkv, (A_QK, A_QK, A_WIDTH))
    q = l2norm(q.reshape(bsz, length, A_HEADS, A_DK)) * (A_DK ** -0.5)
    k = l2norm(k.reshape(bsz, length, A_HEADS, A_DK))
    v = v.reshape(bsz, length, A_HEADS, A_DV).astype(jnp.float32)
    g = -jnp.exp(a_log.astype(jnp.float32)) * jax.nn.softplus(aa.astype(jnp.float32) + a_dt_bias.astype(jnp.float32))
    beta = jax.nn.sigmoid(ab.astype(jnp.float32))
    o_a, new_delta = gated_delta_rule(q, k, v, g, beta, delta_s0.astype(jnp.float32))
    gate = jax.nn.silu(az.reshape(bsz, length, A_HEADS, A_DV).astype(jnp.float32))
    o_a = (rmsnorm(o_a, a_norm) * gate).reshape(bsz, length, A_WIDTH).astype(x.dtype)

    bq5 = bq.reshape(bsz, length, B_HEADS, 2, B_DQK)
    bk5 = bk.reshape(bsz, length, B_HEADS, 2, B_DQK)
    bv4 = bv.reshape(bsz, length, B_HEADS, B_DV)
    lf = b_lambda.astype(jnp.float32)
    lam = jnp.exp(jnp.sum(lf[0] * lf[1])) - jnp.exp(jnp.sum(lf[2] * lf[3])) + lam_init
    if past_k is None:
        o_b = diff_attn_prompt(bq5, bk5, bv4, lam)
    else:
        pk = past_k.reshape(past_k.shape[0], past_k.shape[1], B_HEADS, 2, B_DQK).astype(bk5.dtype)
        k_all = jnp.concatenate([pk, bk5], axis=1)
        v_all = jnp.concatenate([past_v.astype(bv4.dtype), bv4], axis=1)
        o_b = diff_attn_sample(bq5, k_all, v_all, lam)
    o_b = (rmsnorm(o_b, b_norm) * (1.0 - lam_init)).reshape(bsz, length, B_WIDTH).astype(x.dtype)
    new_k = bk.reshape(bsz, length, B_HEADS, 2 * B_DQK)
    new_v = bv4

    o_c, new_pool = pool_mixer(cx, pool_buf, pos0, c_w, c_scale)

    mixed = jnp.concatenate([o_a, o_b, o_c], axis=-1)
    x = x + jnp.einsum('ble,ed->bld', mixed, w_out)
    f, new_ffn = conv_ffn(rmsnorm(x, norm_ffn), ffn_buf, ffn_up, ffn_conv_w, ffn_conv_b, ffn_down)
    x = x + f
    return x, (new_delta.astype(x.dtype), new_conv, new_k, new_v, new_pool, new_ffn)


def setup_inputs(seed: int = 0) -> dict:
    key = jax.random.key(seed)
    ks = jax.random.split(key, 32)
    f32 = jnp.float32

    def nrm(k, shape, s):
        return jax.random.normal(k, shape, f32) * s

    a_log = jnp.log(jax.random.uniform(ks[12], (DEPTH, A_HEADS), f32, 1.0, 16.0))
    dt = jax.random.uniform(ks[13], (DEPTH, A_HEADS), f32, 0.001, 0.1)
    return {
        'x_prompt': nrm(ks[0], (BATCH, SEQ, D_MODEL), 1.0),
        'x_sample': nrm(ks[1], (DEC_BATCH, DEC_SEQ, D_MODEL), 1.0),
        'state_delta': nrm(ks[2], (DEPTH, DEC_BATCH, A_HEADS, A_DK, A_DV), 0.5),
        'cache_qkv_conv': nrm(ks[3], (DEPTH, DEC_BATCH, A_CONV - 1, 3 * A_WIDTH), 1.0),
        'cache_k': nrm(ks[4], (DEPTH, DEC_BATCH, PAST_LEN, B_HEADS, 2 * B_DQK), 1.0),
        'cache_v': nrm(ks[5], (DEPTH, DEC_BATCH, PAST_LEN, B_HEADS, B_DV), 1.0),
        'cache_pool': nrm(ks[6], (DEPTH, DEC_BATCH, C_POOL_BUF, C_WIDTH), 1.0),
        'cache_ffn_conv': nrm(ks[7], (DEPTH, DEC_BATCH, FFN_CONV - 1, 2 * D_FF), 1.0),
        'norm_mix': 1.0 + nrm(ks[8], (DEPTH, D_MODEL), 0.05),
        'w_in': nrm(ks[9], (DEPTH, D_MODEL, IN_WIDTH), D_MODEL ** -0.5),
        'a_conv_w': nrm(ks[10], (DEPTH, A_CONV, 3 * A_WIDTH), A_CONV ** -0.5),
        'a_conv_b': nrm(ks[11], (DEPTH, 3 * A_WIDTH), 0.02),
        'a_log': a_log,
        'a_dt_bias': jnp.log(jnp.expm1(dt)),
        'a_norm': 1.0 + nrm(ks[14], (DEPTH, A_DV), 0.05),
        'b_lambda': nrm(ks[15], (DEPTH, 4, B_DQK), 0.1),
        'b_norm': 1.0 + nrm(ks[16], (DEPTH, B_DV), 0.05),
        'c_w': nrm(ks[17], (DEPTH, C_GROUPS, C_GROUP_DIM, C_GROUP_DIM), C_GROUP_DIM ** -0.5),
        'c_scale': 1.0 + nrm(ks[18], (DEPTH, C_WIDTH), 0.1),
        'w_out': nrm(ks[19], (DEPTH, MIX_WIDTH, D_MODEL), MIX_WIDTH ** -0.5),
        'norm_ffn': 1.0 + nrm(ks[20], (DEPTH, D_MODEL), 0.05),
        'ffn_up': nrm(ks[21], (DEPTH, D_MODEL, 2 * D_FF), D_MODEL ** -0.5),
        'ffn_conv_w': nrm(ks[22], (DEPTH, FFN_CONV, 2 * D_FF), FFN_CONV ** -0.5),
        'ffn_conv_b': nrm(ks[23], (DEPTH, 2 * D_FF), 0.02),
        'ffn_down': nrm(ks[24], (DEPTH, D_FF, D_MODEL), D_FF ** -0.5),
        'norm_final': 1.0 + nrm(ks[25], (D_MODEL,), 0.05),
    }


def reference(x_prompt, x_sample, state_delta, cache_qkv_conv, cache_k, cache_v, cache_pool, cache_ffn_conv,
              norm_mix, w_in, a_conv_w, a_conv_b, a_log, a_dt_bias, a_norm, b_lambda, b_norm,
              c_w, c_scale, w_out, norm_ffn, ffn_up, ffn_conv_w, ffn_conv_b, ffn_down, norm_final):
    yp, ys = x_prompt, x_sample
    nbp = x_prompt.shape[0]
    p_new, s_new = [], []
    for l in range(DEPTH):
        lam_init = 0.8 - 0.6 * math.exp(-0.3 * l)
        w = (norm_mix[l], w_in[l], a_conv_w[l], a_conv_b[l], a_log[l], a_dt_bias[l], a_norm[l],
             b_lambda[l], b_norm[l], c_w[l], c_scale[l], w_out[l],
             norm_ffn[l], ffn_up[l], ffn_conv_w[l], ffn_conv_b[l], ffn_down[l])
        yp, sp = trunk_layer(
            yp, 0,
            jnp.zeros((nbp, A_HEADS, A_DK, A_DV), jnp.float32),
            jnp.zeros((nbp, A_CONV - 1, 3 * A_WIDTH), x_prompt.dtype),
            None, None,
            jnp.zeros((nbp, C_POOL_BUF, C_WIDTH), x_prompt.dtype),
            jnp.zeros((nbp, FFN_CONV - 1, 2 * D_FF), x_prompt.dtype),
            *w, lam_init)
        ys, ss = trunk_layer(
            ys, PAST_LEN, state_delta[l], cache_qkv_conv[l], cache_k[l], cache_v[l],
            cache_pool[l], cache_ffn_conv[l], *w, lam_init)
        p_new.append(sp)
        s_new.append(ss)
    y_prompt = rmsnorm(yp, norm_final)
    y_sample = rmsnorm(ys, norm_final)
    p_delta, p_conv, p_k, p_v, p_pool, p_ffn = [jnp.stack(t) for t in zip(*p_new)]
    s_delta, s_conv, s_k, s_v, s_pool, s_ffn = [jnp.stack(t) for t in zip(*s_new)]
    return (y_prompt, y_sample, p_delta, s_delta, p_conv, s_conv, p_k, s_k, p_v, s_v, p_pool, s_pool, p_ffn, s_ffn)
```

```python
import math
import os
from contextlib import ExitStack
import numpy as np
import concourse.bass as bass
import concourse.mybir as mybir
from concourse.bass_utils import run_bass_kernel_spmd

F32 = mybir.dt.float32
BF16 = mybir.dt.bfloat16
ALU = mybir.AluOpType
AF = mybir.ActivationFunctionType
ESZ = {F32: 4, BF16: 2}
ENGS = ("pe", "act", "dve", "pool", "sp")

D = 1024
NKC = 8
FF = 2816
NJ = 22
EPS = 1e-6
TCH = 1024
C_AQ, C_AK, C_AV, C_AZ, C_AB, C_BQ, C_BK, C_BV, C_CX = 0, 512, 1024, 1536, 2048, 2056, 2568, 3080, 3592
GM, CW, CB, AN, BN, CS, GF, FB, FW, NPRM = 0, 8, 56, 68, 69, 70, 74, 82, 126, 258


PSNAMES = {"ps%d" % i for i in range(8)}


def region(ap):
    if ap.name in PSNAMES:
        return (ap.name, 0, 128, 0, 2048)
    t = ap.tensor
    es = ESZ.get(ap.dtype, 4)
    pstep = 1
    for s in t.shape[1:]:
        pstep *= s
    off = ap.offset
    p_lo = off // pstep
    f_lo = off % pstep
    p_cnt = 1
    f_ext = 0
    for i, (st, cnt) in enumerate(ap.ap):
        if i == 0 and st == pstep:
            p_cnt = cnt
        elif i == 0 and st == 0:
            p_cnt = 1
        else:
            f_ext += abs(st) * (cnt - 1)
    return (ap.name, p_lo, p_lo + p_cnt, f_lo * es, (f_lo + f_ext + 1) * es)


def overlap(a, b):
    return a[1] < b[2] and b[1] < a[2] and a[3] < b[4] and b[3] < a[4]


def covers(a, b):
    return a[1] <= b[1] and a[2] >= b[2] and a[3] <= b[3] and a[4] >= b[4]


class Op:
    __slots__ = ("eng", "fn", "eidx", "waits", "signal", "dma_key", "dma_waits")

    def __init__(self, eng, fn, eidx):
        self.eng = eng
        self.fn = fn
        self.eidx = eidx
        self.waits = {}
        self.dma_waits = {}
        self.signal = False
        self.dma_key = None


class Prog:
    def __init__(self, nc):
        self.nc = nc
        self.estream = {e: [] for e in ENGS}
        self.acc = {}
        self.dma_issued = {}
        self.known = {e: {f: -1 for f in ENGS} for e in ENGS}
        self.known_dma = {e: {} for e in ENGS}
        self.nops = 0

    def _dep_on(self, op, dep):
        if dep is op:
            return
        if dep.dma_key is not None:
            k = dep.dma_key
            cnt = self.dma_issued[k]
            if self.known_dma[op.eng].get(k, 0) >= cnt:
                return
            op.dma_waits[k] = max(op.dma_waits.get(k, 0), cnt)
            return
        f = dep.eng
        if f == "pe" and op.eng == "pe":
            return
        if self.known[op.eng][f] >= dep.eidx:
            return
        op.waits[f] = max(op.waits.get(f, -1), dep.eidx)

    def add(self, eng, fn, reads=(), writes=(), dma_key=None):
        op = Op(eng, fn, len(self.estream[eng]))
        op.dma_key = dma_key
        rregs = [region(a) for a in reads if a is not None]
        wregs = [region(a) for a in writes if a is not None]
        wregs = wregs + [r for r in rregs if r[0] in PSNAMES]
        rregs = [r for r in rregs if r[0] not in PSNAMES]
        for r in rregs:
            lst = self.acc.get(r[0])
            if lst:
                for (reg, dep, isw) in lst:
                    if isw and overlap(reg, r):
                        self._dep_on(op, dep)
        for w in wregs:
            lst = self.acc.get(w[0])
            if lst:
                for (reg, dep, isw) in lst:
                    if overlap(reg, w):
                        self._dep_on(op, dep)
        for f, ei in op.waits.items():
            self.estream[f][ei].signal = True
            if ei > self.known[eng][f]:
                self.known[eng][f] = ei
        for k, c in op.dma_waits.items():
            self.known_dma[eng][k] = c
        if dma_key is not None:
            self.dma_issued[dma_key] = self.dma_issued.get(dma_key, 0) + 1
        for w in wregs:
            lst = self.acc.setdefault(w[0], [])
            lst[:] = [e for e in lst if not covers(w, e[0])]
            lst.append((w, op, True))
        for r in rregs:
            lst = self.acc.setdefault(r[0], [])
            if dma_key is None:
                lst[:] = [e for e in lst if not ((not e[2]) and e[1].eng == eng and e[1].dma_key is None
                                                 and covers(r, e[0]))]
            lst.append((r, op, False))
        self.estream[eng].append(op)
        self.nops += 1
        return op

    def emit(self, final_wait_keys=()):
        nc = self.nc
        with ExitStack() as es:
            sems = {e: es.enter_context(nc.semaphore("s_" + e)) for e in ENGS if e != "sp"}
            dsems = {k: es.enter_context(nc.semaphore("d_%s" % str(k))) for k in self.dma_issued}
            block = es.enter_context(nc.Block())
            sigcount = {}
            for e in ENGS:
                c = 0
                for op in self.estream[e]:
                    if op.signal and op.dma_key is None:
                        c += 1
                    sigcount[(e, op.eidx)] = c

            def run(e, eng):
                for op in self.estream[e]:
                    for f, ei in op.waits.items():
                        eng.wait_ge(sems[f], sigcount[(f, ei)])
                    for k, c in op.dma_waits.items():
                        eng.wait_ge(dsems[k], 16 * c)
                    ins = op.fn(eng)
                    if op.dma_key is not None:
                        ins.then_inc(dsems[op.dma_key], 16)
                    elif op.signal:
                        ins.then_inc(sems[e], 1)
                if e == "sp":
                    for k in final_wait_keys:
                        eng.wait_ge(dsems[k], 16 * self.dma_issued[k])

            @block.tensor
            def _(eng):
                run("pe", eng)

            @block.scalar
            def _(eng):
                run("act", eng)

            @block.vector
            def _(eng):
                run("dve", eng)

            @block.gpsimd
            def _(eng):
                run("pool", eng)

            @block.sync
            def _(eng):
                run("sp", eng)

    def load(self, q, out_sb, in_dram, key):
        return self.add(q, lambda e: e.dma_start(out=out_sb, in_=in_dram), writes=[out_sb], dma_key=key)

    def store(self, q, out_dram, in_sb, key):
        return self.add(q, lambda e: e.dma_start(out=out_dram, in_=in_sb), reads=[in_sb], dma_key=key)

    def mm(self, out, lhsT, rhs, start=True, stop=True):
        return self.add("pe", lambda e: e.matmul(out, lhsT, rhs, start=start, stop=stop),
                        reads=[lhsT, rhs] + ([] if start else [out]), writes=[out])

    def act(self, out, in_, func, bias=None, scale=None, accum_out=None):
        kw = {}
        rd = [in_]
        if bias is not None:
            kw["bias"] = bias
            if not isinstance(bias, (int, float)):
                rd.append(bias)
        if scale is not None:
            kw["scale"] = scale
            if not isinstance(scale, (int, float)):
                rd.append(scale)
        wr = [out]
        if accum_out is not None:
            kw["accum_out"] = accum_out
            wr.append(accum_out)
        return self.add("act", lambda e: e.activation(out, in_, func, **kw), reads=rd, writes=wr)

    def ts(self, eng, out, in0, s1, s2, op0, op1=None):
        rd = [in0]
        if not isinstance(s1, (int, float)):
            rd.append(s1)
        if s2 is not None and not isinstance(s2, (int, float)):
            rd.append(s2)
        if op1 is None:
            return self.add(eng, lambda e: e.tensor_scalar(out, in0, s1, None, op0), reads=rd, writes=[out])
        return self.add(eng, lambda e: e.tensor_scalar(out, in0, s1, s2, op0, op1), reads=rd, writes=[out])

    def tt(self, eng, out, in0, in1, op):
        return self.add(eng, lambda e: e.tensor_tensor(out, in0, in1, op), reads=[in0, in1], writes=[out])

    def stt(self, out, in0, scalar, in1, op0, op1):
        rd = [in0, in1]
        if not isinstance(scalar, (int, float)):
            rd.append(scalar)
        return self.add("dve", lambda e: e.scalar_tensor_tensor(out, in0, scalar, in1, op0, op1),
                        reads=rd, writes=[out])

    def copy(self, eng, out, in_):
        if eng == "act":
            return self.add("act", lambda e: e.copy(out, in_), reads=[in_], writes=[out])
        return self.add(eng, lambda e: e.tensor_copy(out, in_), reads=[in_], writes=[out])

    def memset(self, eng, out, val):
        return self.add(eng, lambda e: e.memset(out, val), writes=[out])

    def recip(self, out, in_):
        return self.add("dve", lambda e: e.reciprocal(out, in_), reads=[in_], writes=[out])


IN_SHAPES = {
    "x_prompt": [2, 2048, D], "x_sample": [2, 64, D],
    "state_delta": [2, 2, 4, 128, 128], "cache_qkv_conv": [2, 2, 3, 1536],
    "cache_k": [2, 2, 2048, 512], "cache_v": [2, 2, 2048, 512],
    "cache_pool": [2, 2, 15, 512], "cache_ffn_conv": [2, 2, 2, 5632],
    "norm_mix": [2, D], "w_in": [2, D, 4104], "a_conv_w": [2, 4, 1536], "a_conv_b": [2, 1536],
    "a_log": [2, 4], "a_dt_bias": [2, 4], "a_norm": [2, 128], "b_lambda": [2, 256], "b_norm": [2, 128],
    "c_w": [2, 4, 128, 128], "c_scale": [2, 512], "w_out": [2, 1536, D], "norm_ffn": [2, D],
    "ffn_up": [2, D, 5632], "ffn_conv_w": [2, 3, 5632], "ffn_conv_b": [2, 5632], "ffn_down": [2, FF, D],
    "norm_final": [D],
}
OUT_SHAPES = {
    "y_p": [2, 2048, D], "y_s": [2, 64, D],
    "st_p": [2, 2, 4, 128, 128], "st_s": [2, 2, 4, 128, 128],
    "cv_p": [2, 2, 3, 1536], "cv_s": [2, 2, 3, 1536],
    "k_p": [2, 2, 2048, 512], "k_s": [2, 2, 64, 512],
    "v_p": [2, 2, 2048, 512], "v_s": [2, 2, 64, 512],
    "pl_p": [2, 2, 15, 512], "pl_s": [2, 2, 15, 512],
    "ff_p": [2, 2, 2, 5632], "ff_s": [2, 2, 2, 5632],
}
OUT_ORDER = ["y_p", "y_s", "st_p", "st_s", "cv_p", "cv_s", "k_p", "k_s", "v_p", "v_s", "pl_p", "pl_s", "ff_p", "ff_s"]


class _Stop(Exception):
    pass


def build_program():
    STOP = int(os.environ.get('KSTOP', '99'))

    def ck(n):
        if STOP == n:
            raise _Stop()

    nc = bass.Bass("TRN2", target_bir_lowering=False)
    I = {k: nc.dram_tensor(k, s, F32, kind="ExternalInput").ap() for k, s in IN_SHAPES.items()}
    O = {k: nc.dram_tensor(k, s, F32, kind="ExternalOutput").ap() for k, s in OUT_SHAPES.items()}
    P = Prog(nc)
    es = ExitStack()
    WSHAPE = {"w_in": (D, 4104), "w_out": (1536, D), "ffn_up": (D, 2 * FF), "ffn_down": (FF, D)}
    SCR = {k: nc.dram_tensor("scr_" + k, [2, r, c], BF16, kind="Internal").ap() for k, (r, c) in WSHAPE.items()}
    sb = lambda n, s, d: es.enter_context(nc.sbuf_tensor(n, s, d))
    x_tm = sb("x_tm", [128, 8, D], F32)
    hT = sb("hT", [128, NKC, TCH], BF16)
    mixT = sb("mixT", [128, 4, TCH], BF16)
    KTc = sb("KTc", [128, 8192], BF16)
    Vc = sb("Vc", [128, 8192], BF16)
    KTn = sb("KTn", [128, 4096], BF16)
    Vn = sb("Vn", [128, 4096], BF16)
    KTn32 = KTn.bitcast(F32)
    Vn32 = Vn.bitcast(F32)
    NSLOT = 4
    wsl = [sb("wsl%d" % i, [128, 4096], BF16) for i in range(NSLOT)]
    ARENA_B = 39936
    ar32 = sb("arena", [128, ARENA_B // 4], F32)
    ar16 = ar32.bitcast(BF16)
    prmT = sb("prmT", [128, 2, NPRM], F32)
    ident_b = sb("ident_b", [128, 128], BF16)
    ident_f = sb("ident_f", [128, 128], F32)
    ones_f = sb("ones_f", [128, 128], F32)
    ones_b = sb("ones_b", [128, 128], BF16)
    TRI = sb("TRI", [128, 128], F32)
    MASKL = sb("MASKL", [128, 128], F32)
    BLK = sb("BLK", [128, 128], F32)
    CHK = sb("CHK", [128, 2, 128], F32)
    chkcol = sb("chkcol", [128, 2], F32)
    gfin = sb("gfin", [128, D], F32)
    cwb = sb("cwb", [128, 2, 4, 128], BF16)
    Sst = sb("Sst", [128, 2, 2, 4, 128], F32)
    Sst16 = Sst.bitcast(BF16)
    ctail = sb("ctail", [128, 2, 2, 3, 12], F32)
    ptail = sb("ptail", [128, 2, 2, 15, 4], F32)
    ftail = sb("ftail", [128, 2, 2, 2, 44], F32)
    invcnt = sb("invcnt", [128, 4, 16], F32)
    bcast = sb("bcast", [128, 2, 12], F32)
    bnS = sb("bnS", [128, 2], F32)
    stg = sb("stg", [128, 512], F32)
    smallA = sb("smallA", [128, 8, 40], F32)
    nrm_s = sb("nrm_s", [128, 16], F32)
    junk = sb("junk", [128, D], BF16)
    xsb = sb("xsb", [128, D], BF16)
    tstg = sb("tstg", [128, 128], F32)
    ystg = None
    ps = [es.enter_context(nc.psum_tensor("ps%d" % i, [128, 512], F32)) for i in range(8)]

    class Ring:
        def __init__(self, tiles):
            self.t = tiles
            self.i = 0

        def get(self):
            r = self.t[self.i % len(self.t)]
            self.i += 1
            return r

    def small_tiles(banks, ncols=128):
        return [ps[b][:, c * ncols:(c + 1) * ncols] for c in range(512 // ncols) for b in banks]

    def A32(off, n):
        assert off % 4 == 0 and off + 4 * n <= ARENA_B, (off, n)
        return ar32[:, off // 4: off // 4 + n]

    def A16(off, n):
        assert off % 2 == 0 and off + 2 * n <= ARENA_B, (off, n)
        return ar16[:, off // 2: off // 2 + n]

    def affsel(out, pattern, cmp, fill, base, cm):
        P.add("pool", lambda e: e.affine_select(out, out, pattern, cmp, fill, base=base, channel_multiplier=cm),
              reads=[out], writes=[out])

    P.memset("pool", ident_f[:], 0.0)
    affsel(ident_f[:], [[-1, 128]], ALU.not_equal, 1.0, 0, 1)
    P.copy("dve", ident_b[:], ident_f[:])
    P.memset("pool", ones_f[:], 1.0)
    P.memset("pool", ones_b[:], 1.0)
    P.memset("pool", TRI[:], 1.0)
    affsel(TRI[:], [[1, 128]], ALU.is_ge, 0.0, 0, -1)
    P.memset("pool", TRI[0:64, 64:128], 0.0)
    P.memset("pool", MASKL[:], 1.0)
    affsel(MASKL[:], [[-1, 128]], ALU.is_gt, 0.0, 0, 1)
    P.memset("pool", MASKL[64:128, 0:64], 0.0)
    P.memset("pool", BLK[:], 0.0)
    P.memset("pool", BLK[0:64, 0:64], 1.0)
    P.memset("pool", BLK[64:128, 64:128], 1.0)
    P.memset("pool", CHK[:], 0.0)
    P.memset("pool", CHK[0:64, 0, :], 1.0)
    P.memset("pool", CHK[64:128, 1, :], 1.0)
    P.memset("pool", chkcol[:], 0.0)
    P.memset("pool", chkcol[0:64, 0:1], 1.0)
    P.memset("pool", chkcol[64:128, 1:2], 1.0)
    for g, win in enumerate((2, 4, 8, 16)):
        P.memset("pool", invcnt[:, g, :], 1.0 / win)
        for t in range(win - 1):
            P.memset("pool", invcnt[:, g, t:t + 1], 1.0 / (t + 1))

    def bcast_row(dst, row_ap, n):
        pt = ps[7][:, 0:n]
        P.mm(pt, ones_f[0:1, :], row_ap)
        P.copy("dve", dst, pt)

    P.load("sp", stg[0:1, 0:512], I["norm_final"][0:512].rearrange("(o n) -> o n", o=1), "stg")
    bcast_row(gfin[:, 0:512], stg[0:1, 0:512], 512)
    P.load("sp", stg[0:1, 0:512], I["norm_final"][512:1024].rearrange("(o n) -> o n", o=1), "stg")
    bcast_row(gfin[:, 512:1024], stg[0:1, 0:512], 512)

    for l in range(2):
        rows = []
        rows.append((I["norm_mix"][l].rearrange("(n p) -> n p", p=128), 8))
        rows.append((I["a_conv_w"][l].rearrange("t (c p) -> (t c) p", p=128), 48))
        rows.append((I["a_conv_b"][l].rearrange("(n p) -> n p", p=128), 12))
        rows.append((I["a_norm"][l].rearrange("(n p) -> n p", p=128), 1))
        rows.append((I["b_norm"][l].rearrange("(n p) -> n p", p=128), 1))
        rows.append((I["c_scale"][l].rearrange("(n p) -> n p", p=128), 4))
        rows.append((I["norm_ffn"][l].rearrange("(n p) -> n p", p=128), 8))
        rows.append((I["ffn_conv_b"][l].rearrange("(n p) -> n p", p=128), 44))
        r0 = 0
        for ap, n in rows:
            P.load("sp", stg[r0:r0 + n, 0:128], ap, "stg")
            r0 += n
        assert r0 == 126
        pt = ps[6][:, 0:126]
        P.mm(pt, stg[0:126, 0:128], ident_f[0:126, 0:126])
        P.copy("dve", prmT[:, l, 0:126], pt)
        fw = I["ffn_conv_w"][l].rearrange("t (c p) -> (t c) p", p=128)
        P.load("sp", stg[0:128, 128:256], fw[0:128, :], "stg")
        pt = ps[6][:, 128:256]
        P.mm(pt, stg[0:128, 128:256], ident_f[:, :])
        P.copy("dve", prmT[:, l, 126:254], pt)
        P.load("sp", stg[0:4, 256:384], fw[128:132, :], "stg")
        pt = ps[6][:, 256:260]
        P.mm(pt, stg[0:4, 256:384], ident_f[0:4, 0:4])
        P.copy("dve", prmT[:, l, 254:258], pt)
        P.load("sp", stg[0:1, 384:388], I["a_log"][l].rearrange("(o n) -> o n", o=1), "stg")
        P.load("sp", stg[0:1, 388:392], I["a_dt_bias"][l].rearrange("(o n) -> o n", o=1), "stg")
        P.load("sp", stg[0:1, 128:384], I["b_lambda"][l].rearrange("(o n) -> o n", o=1), "stg2")
        lam_init = 0.8 - 0.6 * math.exp(-0.3 * l)
        P.tt("dve", stg[0:1, 400:464], stg[0:1, 128:192], stg[0:1, 192:256], ALU.mult)
        P.tt("dve", stg[0:1, 464:512], stg[0:1, 256:304], stg[0:1, 320:368], ALU.mult)
        P.tt("dve", stg[0:1, 128:144], stg[0:1, 304:320], stg[0:1, 368:384], ALU.mult)
        P.add("dve", lambda e: e.reduce_sum(stg[0:1, 392:393], stg[0:1, 400:464], mybir.AxisListType.X),
              reads=[stg[0:1, 400:464]], writes=[stg[0:1, 392:393]])
        P.add("dve", lambda e: e.reduce_sum(stg[0:1, 393:394], stg[0:1, 464:512], mybir.AxisListType.X),
              reads=[stg[0:1, 464:512]], writes=[stg[0:1, 393:394]])
        P.add("dve", lambda e: e.reduce_sum(stg[0:1, 394:395], stg[0:1, 128:144], mybir.AxisListType.X),
              reads=[stg[0:1, 128:144]], writes=[stg[0:1, 394:395]])
        P.tt("dve", stg[0:1, 393:394], stg[0:1, 393:394], stg[0:1, 394:395], ALU.add)
        P.act(stg[0:1, 392:394], stg[0:1, 392:394], AF.Exp)
        P.tt("dve", stg[0:1, 395:396], stg[0:1, 393:394], stg[0:1, 392:393], ALU.subtract)
        P.ts("dve", stg[0:1, 395:396], stg[0:1, 395:396], -lam_init, None, ALU.add)
        P.act(stg[0:1, 384:388], stg[0:1, 384:388], AF.Exp)
        P.ts("dve", stg[0:1, 384:388], stg[0:1, 384:388], -1.0, None, ALU.mult)
        bcast_row(bcast[:, l, 0:12], stg[0:1, 384:396], 12)
        P.ts("dve", bnS[:, l:l + 1], prmT[:, l, BN:BN + 1], 1.0 - lam_init, None, ALU.mult)
        for g in range(4):
            P.load("pool", cwb[:, l, g, :], I["c_w"][l, g], "cw")
    NEGLAM = 11
    CASTN = {}

    def cast_layer(l):
        for k in ("w_in", "w_out", "ffn_up", "ffn_down"):
            rows = WSHAPE[k][0]
            key = "cast_%s_%d" % (k, l)
            r0 = 0
            while r0 < rows:
                n = min(256, rows - r0)
                P.add("pool", (lambda e, k=k, l=l, r0=r0, n=n: e.dma_start(out=SCR[k][l, r0:r0 + n, :], in_=I[k][l, r0:r0 + n, :])),
                      dma_key=key)
                r0 += n
            CASTN[key] = P.dma_issued[key]

    cast_layer(0)
    _cast_state = {"l1_done": False}

    class WStream:
        def __init__(self):
            self.reqs = []
            self.issued = 0
            self.cur = 0

        def plan(self, key, pieces):
            self.reqs.append((key, pieces))

        def _issue(self, idx):
            key, pieces = self.reqs[idx]
            slot = wsl[idx % NSLOT]
            off = 0
            for (wname, l, r0, c0, nk, ncols) in pieces:
                src = SCR[wname][l, r0:r0 + nk * 128, c0:c0 + ncols].rearrange("(k p) c -> p k c", p=128)
                base = slot[:, off:off + 1]
                dst = bass.AP(tensor=base.tensor, offset=base.offset, ap=[[4096, 128], [ncols, nk], [1, ncols]])
                op = P.add("sp", (lambda e, dst=dst, src=src: e.dma_start(out=dst, in_=src)),
                           writes=[slot[:, off:off + nk * ncols]], dma_key="w%d" % (idx % NSLOT))
                ck_ = "cast_%s_%d" % (wname, l)
                if P.known_dma["sp"].get(ck_, 0) < CASTN[ck_]:
                    op.dma_waits[ck_] = CASTN[ck_]
                    P.known_dma["sp"][ck_] = CASTN[ck_]
                off += nk * ncols
            assert off <= 4096

        def get(self, key, ahead=NSLOT - 2):
            assert self.reqs[self.cur][0] == key, (self.reqs[self.cur][0], key)
            while self.issued < min(len(self.reqs), self.cur + ahead + 1):
                self._issue(self.issued)
                self.issued += 1
            slot = wsl[self.cur % NSLOT]
            self.cur += 1
            return slot

    WS = WStream()

    def plan_layer(l, T):
        WS.plan(("ab", l), [("w_in", l, 0, C_AB, 8, 8)])
        for h in range(4):
            WS.plan(("A", l, h), [("w_in", l, 0, c + h * 128, 8, 128) for c in (C_AQ, C_AK, C_AV, C_AZ)])
        WS.plan(("wo", l, 0), [("w_out", l, 0, 0, 4, 1024)])
        WS.plan(("bk", l), [("w_in", l, 0, C_BK, 8, 512)])
        WS.plan(("bv", l), [("w_in", l, 0, C_BV, 8, 512)])
        for h in range(4):
            WS.plan(("B", l, h), [("w_in", l, 0, c + h * 128, 8, 128) for c in (C_BQ, C_BK)])
        WS.plan(("wo", l, 1), [("w_out", l, 512, 0, 4, 1024)])
        WS.plan(("cx", l), [("w_in", l, 0, C_CX, 8, 512)])
        WS.plan(("wo", l, 2), [("w_out", l, 1024, 0, 4, 1024)])
        for ft in range(max(1, T // 512)):
            for jg in range(6):
                nj = 4 if jg < 5 else 2
                WS.plan(("ug", l, ft, jg), [("ffn_up", l, 0, jg * 512, 8, nj * 128)])
                WS.plan(("uv", l, ft, jg), [("ffn_up", l, 0, FF + jg * 512, 8, nj * 128)])
            for jg in range(6):
                nj = 4 if jg < 5 else 2
                WS.plan(("dn", l, ft, jg), [("ffn_down", l, jg * 512, 0, nj, 1024)])

    class Job:
        pass

    jobs = []
    for b in range(2):
        for c in range(2):
            j = Job()
            j.kind, j.b, j.chunk, j.nseq, j.Ls, j.T = "p", b, c, 1, TCH, TCH
            jobs.append(j)
    j = Job()
    j.kind, j.b, j.chunk, j.nseq, j.Ls, j.T = "s", 0, 0, 2, 64, 128
    jobs.append(j)
    for j in jobs:
        for l in range(2):
            plan_layer(l, j.T)

    def rms_to_hT(xtile_fn, ntiles, gcol, l):
        P.memset("pool", nrm_s[:, 0:ntiles], 0.0)
        for i in range(ntiles):
            P.act(junk[:], xtile_fn(i), AF.Square, accum_out=nrm_s[:, i:i + 1])
        P.ts("dve", nrm_s[:, 8:8 + ntiles], nrm_s[:, 0:ntiles], 1.0 / D, EPS, ALU.mult, ALU.add)
        P.act(nrm_s[:, 8:8 + ntiles], nrm_s[:, 8:8 + ntiles], AF.Sqrt)
        P.recip(nrm_s[:, 8:8 + ntiles], nrm_s[:, 8:8 + ntiles])
        KD = os.environ.get('KDBG', 'd')
        if KD == 'a':
            return
        for i in range(ntiles):
            P.act(xsb[:], xtile_fn(i), AF.Identity, scale=nrm_s[:, 8 + i:9 + i])
            if KD == 'b':
                continue
            for half in range(2):
                bank = ps[half]
                for q in range(4):
                    kc = half * 4 + q
                    P.mm(bank[:, q * 128:(q + 1) * 128], xsb[:, kc * 128:(kc + 1) * 128], ident_b[:])
                if KD == 'c':
                    continue
                for q in range(4):
                    kc = half * 4 + q
                    eng = "dve" if q % 2 == 0 else "pool"
                    if KD == 'e':
                        eng = "dve"
                    if KD == 'f':
                        eng = "pool"
                    if KD == 'g':
                        P.copy("dve", hT[:, kc, i * 128:(i + 1) * 128], bank[:, q * 128:(q + 1) * 128])
                        continue
                    if KD == 'h':
                        P.ts("dve", junk[:, kc * 128:(kc + 1) * 128], bank[:, q * 128:(q + 1) * 128],
                             prmT[:, l, gcol + kc: gcol + kc + 1], None, ALU.mult)
                        continue
                    if eng == "pool":
                        P.act(hT[:, kc, i * 128:(i + 1) * 128], bank[:, q * 128:(q + 1) * 128], AF.Identity,
                              scale=prmT[:, l, gcol + kc: gcol + kc + 1])
                    else:
                        P.ts("dve", hT[:, kc, i * 128:(i + 1) * 128], bank[:, q * 128:(q + 1) * 128],
                             prmT[:, l, gcol + kc: gcol + kc + 1], None, ALU.mult)

    def rsqrt_ps(dst32, src_ps, scale, n):
        P.ts("dve", dst32, src_ps, scale, EPS, ALU.mult, ALU.add)
        P.act(dst32, dst32, AF.Ln)
        P.act(dst32, dst32, AF.Exp, scale=-0.5)

    def tok_ranges(job, t0, n):
        out = []
        t = t0
        while t < t0 + n:
            s = t // job.Ls
            e = min(t0 + n, (s + 1) * job.Ls)
            out.append((s, t - s * job.Ls, t - t0, e - t))
            t = e
        return out

    def store_tails(tail_ap2d, ncols, dram2d, key):
        pt = ps[7][0:ncols, 128:256]
        P.mm(pt, tail_ap2d, ident_f[:, :])
        P.copy("dve", tstg[0:ncols, :], pt)
        P.store("act", dram2d, tstg[0:ncols, :], key)

    def load_tails(tail_ap2d, ncols, dram2d):
        P.load("sp", tstg[0:ncols, :], dram2d, "tstg_l")
        pt = ps[7][:, 256:256 + ncols]
        P.mm(pt, tstg[0:ncols, :], ident_f[0:ncols, 0:ncols])
        P.copy("dve", tail_ap2d, pt)

    def layer_step(job, l):
        T, nseq, Ls = job.T, job.nseq, job.Ls
        NT = T // 128
        TN = min(512, T)
        NTT = T // TN
        first = (job.kind == "p" and job.chunk == 0)
        last = (job.kind == "s") or (job.chunk == 1)
        bb = lambda s: (job.b if job.kind == "p" else s)
        ss_ = lambda s: (0 if job.kind == "p" else s)
        OP = (lambda k: O[k + "_p"]) if job.kind == "p" else (lambda k: O[k + "_s"])
        pr = lambda c: prmT[:, l, c:c + 1]

        rms_to_hT(lambda i: x_tm[:, i, :], NT, GM, l)
        ck(1)

        wab = WS.get(("ab", l))
        SM = smallA
        def gates_gen():
            for i in range(NT):
                pt = ps[7][:, 0:8]
                for kc in range(8):
                    P.mm(pt, hT[:, kc, i * 128:(i + 1) * 128], wab[:, kc * 8:(kc + 1) * 8], start=(kc == 0), stop=(kc == 7))
                P.copy("dve", SM[:, i, 0:8], pt)
                yield
                P.tt("dve", SM[:, i, 8:12], SM[:, i, 0:4], bcast[:, l, 4:8], ALU.add)
                P.act(SM[:, i, 8:12], SM[:, i, 8:12], AF.Exp)
                yield
                P.ts("dve", SM[:, i, 8:12], SM[:, i, 8:12], 1.0, None, ALU.add)
                P.act(SM[:, i, 8:12], SM[:, i, 8:12], AF.Ln)
                yield
                P.tt("dve", SM[:, i, 8:12], SM[:, i, 8:12], bcast[:, l, 0:4], ALU.mult)
                P.act(SM[:, i, 12:16], SM[:, i, 4:8], AF.Exp, scale=-1.0)
                yield
                P.ts("dve", SM[:, i, 12:16], SM[:, i, 12:16], 1.0, None, ALU.add)
                P.recip(SM[:, i, 12:16], SM[:, i, 12:16])
                yield
                P.ts("dve", SM[:, i, 16:20], SM[:, i, 12:16], -1.0, None, ALU.mult)
                pg = ps[7][:, 16:32]
                P.mm(pg[:, 0:4], TRI[:], SM[:, i, 8:12])
                P.mm(pg[:, 4:8], BLK[:], SM[:, i, 8:12])
                P.mm(pg[:, 8:12], CHK[:, 0, :], SM[:, i, 8:12])
                P.mm(pg[:, 12:16], CHK[:, 1, :], SM[:, i, 8:12])
                P.copy("dve", SM[:, i, 20:36], pg)
                yield
                P.tt("dve", SM[:, i, 24:28], SM[:, i, 24:28], SM[:, i, 20:24], ALU.subtract)
                P.act(SM[:, i, 24:36], SM[:, i, 24:36], AF.Exp)
                yield
                P.act(SM[:, i, 36:40], SM[:, i, 20:24], AF.Exp)
                yield
                P.tt("dve", SM[:, i, 36:40], SM[:, i, 36:40], SM[:, i, 12:16], ALU.mult)
        ck(2)
        XW = nseq * (3 + Ls)
        o = 0
        xraw = A32(o, XW); o += 4 * ((XW + 31) // 32 * 32)
        ycv = A32(o, T); o += 4 * T
        oT = A32(o, T); o += 4 * T
        rt = [A32(o, 512), A32(o + 2048, 512)]; o += 4096
        f32t = [A32(o + 512 * k, 128) for k in range(21)]; o += 512 * 21
        sqb = A16(o, T); o += 2 * T
        qT = A16(o, T); o += 2 * T
        kT = A16(o, T); o += 2 * T
        vT = A16(o, T); o += 2 * T
        zT = A16(o, T); o += 2 * T
        b16t = [A16(o + 256 * k, 128) for k in range(8)]; o += 256 * 8
        assert o <= ARENA_B, o
        xf32 = [KTn32[:, 128 * k:128 * (k + 1)] for k in range(16)] + [Vn32[:, 128 * k:128 * (k + 1)] for k in range(4)]
        xb16 = [Vn[:, 1024 + 128 * k:1024 + 128 * (k + 1)] for k in range(8)]
        tdelta = f32t[8]
        ints = [f32t[0:8], xf32[0:8]]
        outs = [f32t[9:15], f32t[15:21], xf32[8:14], xf32[14:20]]
        b16s = [b16t, xb16]
        smr = Ring(small_tiles([2, 3, 4, 5, 6, 7]))
        bigr = Ring([ps[0][:], ps[1][:]])
        P.memset("pool", tdelta, 0.0)

        for s in range(nseq):
            if first:
                P.memset("pool", Sst[:, l, ss_(s), :, :], 0.0)
                P.memset("pool", ctail[:, l, ss_(s), :, :], 0.0)
            elif job.kind == "s":
                for h in range(4):
                    P.load("sp", Sst[:, l, s, h, :], I["state_delta"][l, s, h], "sst")
                load_tails(ctail[:, l, s, :, :], 36, I["cache_qkv_conv"][l, s].rearrange("t (c p) -> (t c) p", p=128))

        sets_dbg = os.environ.get('KSETS', '2')
        _b0 = Sst16[:, 0, 1, 0, 0:1]
        Sst16_q = bass.AP(tensor=_b0.tensor, offset=_b0.offset, ap=[[4096, 128], [1, 1024]])
        _b1 = Sst16[:, 1, 1, 0, 0:1]
        Sst16_k = bass.AP(tensor=_b1.tensor, offset=_b1.offset, ap=[[4096, 128], [1, 1024]])
        sets = [(qT, kT, vT, zT),
                (Sst16_q[:, 0:T], Sst16_k[:, 0:T], Vn[:, 2048:2048 + T], Vn[:, 3072:3072 + T])]
        if sets_dbg == '1' or NT == 1:
            sets = [sets[0], sets[0]]

        def zipgen(*gens):
            gens = list(gens)
            while gens:
                for g in list(gens):
                    try:
                        next(g)
                        yield
                    except StopIteration:
                        gens.remove(g)

        def chain(*gs):
            for g in gs:
                yield from g

        def run(g):
            for _ in g:
                pass

        def stageC(h):
            qT_, kT_, vT_, zT_ = sets[h % 2]
            wA = WS.get(("A", l, h))
            for pi, (dst, cidx) in enumerate(((qT_, h), (kT_, 4 + h), (vT_, 8 + h), (zT_, None))):
                for tt_ in range(NTT):
                    pt = bigr.get()[:, 0:TN]
                    for kc in range(8):
                        P.mm(pt, wA[:, (pi * 8 + kc) * 128:(pi * 8 + kc + 1) * 128], hT[:, kc, tt_ * TN:(tt_ + 1) * TN],
                             start=(kc == 0), stop=(kc == 7))
                        if kc % 2 == 1:
                            yield
                    if cidx is None:
                        P.act(zT_[:, tt_ * TN:(tt_ + 1) * TN], pt, AF.Silu)
                    else:
                        for (s, ts_, co, n) in tok_ranges(job, tt_ * TN, TN):
                            base = s * (3 + Ls) + 3 + ts_
                            P.copy("act", xraw[:, base:base + n], pt[:, co:co + n])
                    yield
                if cidx is None:
                    continue
                for s in range(nseq):
                    base = s * (3 + Ls)
                    P.copy("pool", xraw[:, base:base + 3], ctail[:, l, ss_(s), :, cidx])
                    yv = ycv[:, s * Ls:(s + 1) * Ls]
                    P.ts("dve", yv, xraw[:, base + 3:base + 3 + Ls], pr(CW + 3 * 12 + cidx), pr(CB + cidx), ALU.mult, ALU.add)
                    yield
                    for tap in (2, 1, 0):
                        P.stt(yv, xraw[:, base + tap:base + tap + Ls], pr(CW + tap * 12 + cidx), yv, ALU.mult, ALU.add)
                        yield
                    P.copy("pool", ctail[:, l, ss_(s), :, cidx], xraw[:, base + Ls:base + Ls + 3])
                P.act(ycv[:, 0:T], ycv[:, 0:T], AF.Silu)
                yield
                if pi == 2:
                    P.copy("pool", vT_[:, 0:T], ycv[:, 0:T])
                    yield
                else:
                    P.act(sqb[:, 0:T], ycv[:, 0:T], AF.Square)
                    yield
                    for tt_ in range(NTT):
                        pt = bigr.get()[:, 0:TN]
                        P.mm(pt, ones_b[:], sqb[:, tt_ * TN:(tt_ + 1) * TN])
                        r = rt[tt_ % 2][:, 0:TN]
                        P.ts("dve", r, pt, 1.0, EPS, ALU.mult, ALU.add)
                        yield
                        P.act(r, r, AF.Ln)
                        yield
                        P.act(r, r, AF.Exp, scale=-0.5)
                        yield
                        if pi == 0:
                            P.stt(dst[:, tt_ * TN:(tt_ + 1) * TN], ycv[:, tt_ * TN:(tt_ + 1) * TN], 128 ** -0.5, r, ALU.mult, ALU.mult)
                        else:
                            P.tt("dve", dst[:, tt_ * TN:(tt_ + 1) * TN], ycv[:, tt_ * TN:(tt_ + 1) * TN], r, ALU.mult)
                        yield

        def pre_gen(h, i):
            qT_, kT_, vT_, zT_ = sets[h % 2]
            tl = slice(i * 128, (i + 1) * 128)
            sc = lambda c: SM[:, i, c:c + 1]
            tU, tWT, tqg, tQKm, tKt0, tKt1 = outs[i % 4]
            (tR, tD1, tD2, tdec, tdecT, tgam, tdecm, tdecTm) = ints[i % 2]
            (bA0, bA1, bB0, bB1, bX0, bX1, bKBG, bVB) = b16s[i % 2]
            P.ts("pool", tR, TRI[:], sc(8 + h), None, ALU.mult)
            pG = smr.get()
            P.mm(pG, ones_f[:], tR)
            P.ts("dve", tD1, pG, sc(20 + h), 0.0, ALU.subtract, ALU.max)
            P.ts("dve", tD2, pG, sc(20 + h), 0.0, ALU.subtract, ALU.min)
            yield
            P.act(tdec, tD1, AF.Exp, scale=-1.0)
            P.act(tdecT, tD2, AF.Exp)
            P.act(tgam, pG, AF.Exp)
            yield
            pkk = smr.get()
            P.mm(pkk, kT_[:, tl], kT_[:, tl])
            pqk = smr.get()
            P.mm(pqk, kT_[:, tl], qT_[:, tl])
            P.tt("pool", tdecm, tdec, MASKL[:], ALU.mult)
            P.tt("pool", tdecTm, tdecT, TRI[:], ALU.mult)
            yield
            P.stt(bA0, pkk, sc(16 + h), tdecm, ALU.mult, ALU.mult)
            P.tt("dve", tQKm, pqk, tdecTm, ALU.mult)
            P.tt("pool", tqg, qT_[:, tl], tgam, ALU.mult)
            yield
            pB = smr.get()
            P.mm(pB, bA0, ident_b[:])
            P.copy("act", bB0, pB)
            P.tt("dve", bX0, pB, ident_f[:], ALU.add)
            yield
            Ac, Bc, Xc = bA0, bB0, bX0
            An, Bn, Xn = bA1, bB1, bX1
            for k in range(1, 6):
                pA = smr.get()
                P.mm(pA, Bc, Ac)
                P.copy("act", An, pA)
                yield
                if k < 5:
                    pBn = smr.get()
                    P.mm(pBn, Ac, Bc)
                    P.copy("act", Bn, pBn)
                    yield
                pX = smr.get()
                P.mm(pX, An, Xc)
                P.tt("dve", Xn, pX, Xc, ALU.add)
                yield
                Ac, An = An, Ac
                Bc, Bn = Bn, Bc
                Xc, Xn = Xn, Xc
            pkt = smr.get()
            P.mm(pkt, kT_[:, tl], ident_b[:])
            P.ts("dve", bKBG, pkt, sc(36 + h), None, ALU.mult)
            P.ts("dve", tKt0, pkt, sc(24 + h), chkcol[:, 0:1], ALU.mult, ALU.mult)
            P.ts("dve", tKt1, pkt, sc(24 + h), chkcol[:, 1:2], ALU.mult, ALU.mult)
            yield
            pvt = smr.get()
            P.mm(pvt, vT_[:, tl], ident_b[:])
            P.ts("dve", bVB, pvt, sc(12 + h), None, ALU.mult)
            yield
            pU = smr.get()
            P.mm(pU, Xc, bVB)
            P.copy("act", tU, pU)
            pW = smr.get()
            P.mm(pW, bKBG, Xc)
            P.copy("act", tWT, pW)
            yield

        def scan_gen(h, i):
            sc = lambda c: SM[:, i, c:c + 1]
            tU, tWT, tqg, tQKm, tKt0, tKt1 = outs[i % 4]
            for jc in range(2):
                seq = (i * 128 + jc * 64) // Ls
                S = Sst[:, l, ss_(seq), h, :]
                cs = slice(jc * 64, (jc + 1) * 64)
                pWS = smr.get()
                P.mm(pWS, tWT, S)
                pO = smr.get()[:, 0:64]
                P.mm(pO, S, tqg[:, cs], start=True, stop=False)
                yield
                P.tt("dve", tdelta, tU, pWS, ALU.subtract)
                yield
                P.mm(pO, tdelta, tQKm[:, cs], start=False, stop=True)
                pSU = smr.get()
                P.mm(pSU, (tKt0, tKt1)[jc], tdelta)
                yield
                P.stt(S, S, sc(28 + 4 * jc + h), pSU, ALU.mult, ALU.add)
                P.copy("act", oT[:, i * 128 + jc * 64: i * 128 + (jc + 1) * 64], pO)
                yield

        def norm_gen(h):
            qT_, kT_, vT_, zT_ = sets[h % 2]
            P.act(sqb[:, 0:T], oT[:, 0:T], AF.Square)
            yield
            for tt_ in range(NTT):
                sl_ = slice(tt_ * TN, (tt_ + 1) * TN)
                pt = bigr.get()[:, 0:TN]
                P.mm(pt, ones_b[:], sqb[:, sl_])
                r = rt[tt_ % 2][:, 0:TN]
                P.ts("dve", r, pt, 1.0 / 128, EPS, ALU.mult, ALU.add)
                yield
                P.act(r, r, AF.Ln)
                yield
                P.act(r, r, AF.Exp, scale=-0.5)
                yield
                P.stt(r, oT[:, sl_], pr(AN), r, ALU.mult, ALU.mult)
                yield
                P.tt("dve", mixT[:, h, sl_], r, zT_[:, sl_], ALU.mult)
                yield
            if last:
                for s in range(nseq):
                    P.store("act", OP("st")[l, bb(s), h], Sst[:, l, ss_(s), h, :], "st_out")

        def stageW(h):
            if NT == 1:
                yield from pre_gen(h, 0)
                yield from scan_gen(h, 0)
            else:
                yield from zipgen(pre_gen(h, 0), pre_gen(h, 1))
                for p_ in range(NT // 2):
                    gl_ = [chain(scan_gen(h, 2 * p_), scan_gen(h, 2 * p_ + 1))]
                    if 2 * p_ + 2 < NT:
                        gl_ += [pre_gen(h, 2 * p_ + 2), pre_gen(h, 2 * p_ + 3)]
                    yield from zipgen(*gl_)
            yield from norm_gen(h)

        run(zipgen(stageC(0), gates_gen()))
        ck(3)
        for h in range(4):
            gl2 = [stageW(h)]
            if h + 1 < 4:
                gl2.append(stageC(h + 1))
            if os.environ.get("KZIP", "1") == "0" or NT == 1:
                for g_ in gl2:
                    run(g_)
            else:
                run(zipgen(*gl2))
        ck(4)
        if last:
            for s in range(nseq):
                store_tails(ctail[:, l, ss_(s), :, :].rearrange("p t c -> p (t c)"), 36,
                            OP("cv")[l, bb(s)].rearrange("t (c p) -> (t c) p", p=128), "tl_out")

        def out_proj_gen(wo, banks):
            rr_ = Ring([ps[b_][:] for b_ in banks])
            for i in range(NT):
                for half in range(2):
                    pt = rr_.get()
                    for kc in range(4):
                        P.mm(pt, mixT[:, kc, i * 128:(i + 1) * 128], wo[:, kc * 1024 + half * 512: kc * 1024 + half * 512 + 512],
                             start=(kc == 0), stop=(kc == 3))
                    P.tt("dve", x_tm[:, i, half * 512:(half + 1) * 512], x_tm[:, i, half * 512:(half + 1) * 512], pt, ALU.add)
                    yield

        wo0 = WS.get(("wo", l, 0))
        opg0 = out_proj_gen(wo0, [6, 7])

        o = 0
        QTall = [A16(o + 2 * T * hh, T) for hh in range(4)]; o += 8 * T
        PT = [[A16(o + 1024 * (2 * bfi + m), 512) for m in range(2)] for bfi in range(3)]; o += 6144
        sq2 = A16(o, 512); o += 1024
        rd = A32(o, 512); o += 2048
        o1 = A32(o, 512); o += 2048
        o2 = A32(o, 512); o += 2048
        ktok = [A32(o, 512), A32(o + 2048, 512)]; o += 4096
        vtok = [A32(o, 512), A32(o + 2048, 512)]; o += 4096
        sO = [A32(o, 512), A32(o + 2048, 512)]; o += 4096
        sD = [A32(o, 512), A32(o + 2048, 512)]; o += 4096
        assert o <= ARENA_B
        rr = Ring([ps[0][:], ps[1][:]])
        wbk = WS.get(("bk", l))
        wbv = WS.get(("bv", l), ahead=1)
        if job.kind == "p" and job.chunk == 0:
            KTnew = lambda h: KTc[:, (l * 4 + h) * 1024:(l * 4 + h) * 1024 + T]
            Vnew = lambda i, h: Vc[:, (l * 8 + i) * 512 + h * 128:(l * 8 + i) * 512 + (h + 1) * 128]
            Vnew_full = lambda i: Vc[:, (l * 8 + i) * 512:(l * 8 + i + 1) * 512]
        else:
            KTnew = lambda h: KTn[:, h * 1024:h * 1024 + T]
            Vnew = lambda i, h: Vn[:, i * 512 + h * 128:i * 512 + (h + 1) * 128]
            Vnew_full = lambda i: Vn[:, i * 512:(i + 1) * 512]
        def kv_gen():
            for i in range(NT):
                for wsel, tokb, oname in ((wbk, ktok, "k"), (wbv, vtok, "v")):
                    pt = rr.get()
                    for kc in range(8):
                        P.mm(pt, hT[:, kc, i * 128:(i + 1) * 128], wsel[:, kc * 512:(kc + 1) * 512], start=(kc == 0), stop=(kc == 7))
                    tb = tokb[i % 2]
                    P.copy("act", tb, pt)
                    if oname == "v":
                        P.copy("dve", Vnew_full(i), pt)
                    for (s, ts_, co, n) in tok_ranges(job, i * 128, 128):
                        t_abs = ts_ + (job.chunk * TCH if job.kind == "p" else 0)
                        P.store("act", OP(oname)[l, bb(s), t_abs:t_abs + n, :], tb[co:co + n, :], "%stok%d" % (oname, i % 2))
                    yield

        run(zipgen(opg0, kv_gen()))
        ck(5)
        ck(6)
        def load_cache(s):
            for kb in range(16):
                st_ = vtok[kb % 2]
                P.load("sp", st_, I["cache_v"][l, s, kb * 128:(kb + 1) * 128, :], "cv_l%d" % (kb % 2))
                P.copy("dve" if kb % 2 else "pool", Vc[:, kb * 512:(kb + 1) * 512], st_)
            for kb in range(16):
                st_ = ktok[kb % 2]
                P.load("sp", st_, I["cache_k"][l, s, kb * 128:(kb + 1) * 128, :], "ck_l%d" % (kb % 2))
                pt = rr.get()
                for h in range(4):
                    P.mm(pt[:, h * 128:(h + 1) * 128], st_[:, h * 128:(h + 1) * 128], ident_f[:])
                for h in range(4):
                    P.copy("act" if h % 2 else "dve", KTc[:, h * 2048 + kb * 128: h * 2048 + (kb + 1) * 128], pt[:, h * 128:(h + 1) * 128])

        accO = [ps[2][:], ps[3][:]]
        accD = [ps[4][:], ps[5][:]]
        sring = Ring([ps[0][:], ps[1][:], ps[6][:], ps[7][:]])
        for h in range(4):
            wB = WS.get(("B", l, h))
            for pi in range(2):
                dst = QTall[h] if pi == 0 else KTnew(h)
                for tt_ in range(NTT):
                    pt = sring.get()[:, 0:TN]
                    for kc in range(8):
                        P.mm(pt, wB[:, (pi * 8 + kc) * 128:(pi * 8 + kc + 1) * 128], hT[:, kc, tt_ * TN:(tt_ + 1) * TN],
                             start=(kc == 0), stop=(kc == 7))
                    P.copy("act", dst[:, tt_ * TN:(tt_ + 1) * TN], pt)
        def blocks_gen(s, h, qt):
            QT = QTall[h]
            ktn = KTnew(h)
            nq = min(512, Ls)
            q0 = s * Ls + qt * 512
            blocks = []
            if job.kind == "s":
                for kb in range(16):
                    blocks.append((KTc[:, h * 2048 + kb * 128: h * 2048 + (kb + 1) * 128],
                                   Vc[:, kb * 512 + h * 128: kb * 512 + (h + 1) * 128], 0, nq, False, None))
                blocks.append((ktn[:, 0:128], Vnew(0, h), 0, nq, False, s))
            else:
                if job.chunk == 1:
                    for kb in range(8):
                        blocks.append((KTc[:, (l * 4 + h) * 1024 + kb * 128:(l * 4 + h) * 1024 + (kb + 1) * 128],
                                       Vc[:, (l * 8 + kb) * 512 + h * 128:(l * 8 + kb) * 512 + (h + 1) * 128], 0, nq, False, None))
                for kb in range(4 * qt + 4):
                    qlo = max(qt * 512, kb * 128)
                    blocks.append((ktn[:, kb * 128:(kb + 1) * 128], Vnew(kb, h), qlo - qt * 512, qt * 512 + 512 - qlo,
                                   kb >= 4 * qt, None))
            nb = len(blocks)
            pend = None

            def flush(pend_):
                for (m_, pt__, v__, qoff_, n_, bi_) in pend_:
                    P.mm(accO[m_][:, qoff_:qoff_ + n_], v__, pt__, start=(bi_ == 0), stop=(bi_ == nb - 1))
                    P.mm(accD[m_][:, qoff_:qoff_ + n_], ones_b[:], pt__, start=(bi_ == 0), stop=(bi_ == nb - 1))

            for bi, (kt_ap, v_ap, qoff, n, diag, rowmask) in enumerate(blocks):
                cur = []
                for m in range(2):
                    pS = sring.get()[:, 0:n]
                    P.mm(pS, kt_ap[64 * m:64 * m + 64, :], QT[64 * m:64 * m + 64, q0 + qoff:q0 + qoff + n])
                    pt_ = PT[bi % 3][m][:, 0:n]
                    P.act(pt_, pS, AF.Exp, scale=0.125)
                    if diag:
                        P.memset("pool", pt_[64:128, 0:64], 0.0)
                    if rowmask is not None:
                        other = 1 - rowmask
                        P.memset("pool", pt_[64 * other:64 * other + 64, :], 0.0)
                    cur.append((m, pt_, v_ap, qoff, n, bi))
                if pend is not None:
                    flush(pend)
                pend = cur
                yield
            flush(pend)
            P.copy("act", sO[0][:, 0:nq], accO[0][:, 0:nq])
            P.copy("dve", sD[0][:, 0:nq], accD[0][:, 0:nq])
            yield
            P.copy("act", sO[1][:, 0:nq], accO[1][:, 0:nq])
            P.copy("dve", sD[1][:, 0:nq], accD[1][:, 0:nq])
            yield

        def combine_gen(s, h, qt):
            nq = min(512, Ls)
            q0 = s * Ls + qt * 512
            P.act(rd[:, 0:nq], sD[0][:, 0:nq], AF.Ln)
            yield
            P.act(rd[:, 0:nq], rd[:, 0:nq], AF.Exp, scale=-1.0)
            yield
            P.tt("dve", o1[:, 0:nq], sO[0][:, 0:nq], rd[:, 0:nq], ALU.mult)
            yield
            P.act(rd[:, 0:nq], sD[1][:, 0:nq], AF.Ln)
            yield
            P.act(rd[:, 0:nq], rd[:, 0:nq], AF.Exp, scale=-1.0)
            yield
            P.tt("dve", o2[:, 0:nq], sO[1][:, 0:nq], rd[:, 0:nq], ALU.mult)
            yield
            P.stt(o1[:, 0:nq], o2[:, 0:nq], bcast[:, l, NEGLAM:NEGLAM + 1], o1[:, 0:nq], ALU.mult, ALU.add)
            yield
            P.act(sq2[:, 0:nq], o1[:, 0:nq], AF.Square)
            yield
            pn = sring.get()[:, 0:nq]
            P.mm(pn, ones_b[:], sq2[:, 0:nq])
            P.ts("dve", rd[:, 0:nq], pn, 1.0 / 128, EPS, ALU.mult, ALU.add)
            yield
            P.act(rd[:, 0:nq], rd[:, 0:nq], AF.Ln)
            yield
            P.act(rd[:, 0:nq], rd[:, 0:nq], AF.Exp, scale=-0.5)
            yield
            P.stt(mixT[:, h, q0:q0 + nq], o1[:, 0:nq], bnS[:, l:l + 1], rd[:, 0:nq], ALU.mult, ALU.mult)
            yield

        def runB(g_):
            for _ in g_:
                pass

        def zipB(*gens):
            gens = list(gens)
            while gens:
                for g_ in list(gens):
                    try:
                        next(g_)
                    except StopIteration:
                        gens.remove(g_)

        for s in range(nseq):
            if job.kind == "s":
                load_cache(s)
            groups = [(s, h, qt) for h in range(4) for qt in range(max(1, Ls // 512))]
            if job.kind == "s":
                for gq in groups:
                    runB(blocks_gen(*gq))
                    runB(combine_gen(*gq))
            else:
                runB(blocks_gen(*groups[0]))
                for gi in range(len(groups)):
                    if gi + 1 < len(groups):
                        zipB(combine_gen(*groups[gi]), blocks_gen(*groups[gi + 1]))
                    else:
                        runB(combine_gen(*groups[gi]))
        wo1 = WS.get(("wo", l, 1))
        op1_state = {"done": False}

        def opg1_wrap():
            yield from out_proj_gen(wo1, [6, 7])
            op1_state["done"] = True

        opg1 = opg1_wrap()
        ck(7)

        o = 0
        XC = nseq * (15 + Ls)
        XCB = 4 * ((XC + 31) // 32 * 32)
        cbufs = []
        for _cb in range(2):
            xc_ = A32(o, XC); o += XCB
            pa_ = A32(o, XC); o += XCB
            pbb_ = A32(o, XC); o += XCB
            pgb_ = A16(o, T); o += 2 * T
            t16_ = A32(o, 16); o += 64
            cbufs.append((xc_, pa_, pbb_, pgb_, t16_))
        assert o <= ARENA_B, o
        wcx = WS.get(("cx", l))
        rr = Ring([ps[0][:], ps[1][:]])
        if job.kind == "s":
            for s in range(nseq):
                load_tails(ptail[:, l, s, :, :].rearrange("p t c -> p (t c)"), 60,
                           I["cache_pool"][l, s].rearrange("t (c p) -> (t c) p", p=128))

        def zipgenC(*gens):
            gens = list(gens)
            while gens:
                for g_ in list(gens):
                    try:
                        next(g_)
                    except StopIteration:
                        gens.remove(g_)

        def Cgen(g, win):
            xc, pa, pb_, pgb, t16 = cbufs[g % 2]
            for tt_ in range(NTT):
                pt = rr.get()[:, 0:TN]
                for kc in range(8):
                    P.mm(pt, wcx[:, kc * 512 + g * 128: kc * 512 + (g + 1) * 128], hT[:, kc, tt_ * TN:(tt_ + 1) * TN],
                         start=(kc == 0), stop=(kc == 7))
                yield
                for (s, ts_, co, n) in tok_ranges(job, tt_ * TN, TN):
                    base = s * (15 + Ls) + 15 + ts_
                    P.copy("act", xc[:, base:base + n], pt[:, co:co + n])
                yield
            for s in range(nseq):
                base = s * (15 + Ls)
                N = 15 + Ls
                if first:
                    P.memset("pool", xc[:, base:base + 15], 0.0)
                else:
                    P.copy("pool", xc[:, base:base + 15], ptail[:, l, ss_(s), :, g])
                e_ = lambda a, b_: xc[:, base + a:base + b_]
                A_ = lambda a, b_: pa[:, base + a:base + b_]
                B_ = lambda a, b_: pb_[:, base + a:base + b_]
                P.tt("pool", A_(1, N), e_(1, N), e_(0, N - 1), ALU.add)
                yield
                cur = A_
                if g >= 1:
                    P.tt("pool", B_(3, N), A_(3, N), A_(1, N - 2), ALU.add)
                    cur = B_
                    yield
                if g >= 2:
                    P.tt("pool", A_(7, N), B_(7, N), B_(3, N - 4), ALU.add)
                    cur = A_
                    yield
                if g >= 3:
                    P.tt("pool", B_(15, N), A_(15, N), A_(7, N - 8), ALU.add)
                    cur = B_
                    yield
                P.stt(pgb[:, s * Ls:(s + 1) * Ls], cur(15, N), 1.0 / win, e_(15, N), ALU.mult, ALU.subtract)
                if first:
                    P.tt("dve", t16, cur(15, 31), invcnt[:, g, :], ALU.mult)
                    P.tt("dve", pgb[:, s * Ls:s * Ls + 16], t16, e_(15, 31), ALU.subtract)
                P.copy("pool", ptail[:, l, ss_(s), :, g], e_(Ls, Ls + 15))
                yield
            while not op1_state["done"]:
                yield
            for tt_ in range(NTT):
                pt = rr.get()[:, 0:TN]
                P.mm(pt, cwb[:, l, g, :], pgb[:, tt_ * TN:(tt_ + 1) * TN])
                P.act(mixT[:, g, tt_ * TN:(tt_ + 1) * TN], pt, AF.Identity, scale=pr(CS + g))
                yield

        wins = (2, 4, 8, 16)
        run(zipgen(opg1, chain(zipgen(Cgen(0, wins[0]), Cgen(1, wins[1])), zipgen(Cgen(2, wins[2]), Cgen(3, wins[3])))))
        if last:
            for s in range(nseq):
                store_tails(ptail[:, l, ss_(s), :, :].rearrange("p t c -> p (t c)"), 60,
                            OP("pl")[l, bb(s)].rearrange("t (c p) -> (t c) p", p=128), "tl_out")
        run(out_proj_gen(WS.get(("wo", l, 2)), [0, 1]))
        ck(8)

        if not _cast_state["l1_done"]:
            cast_layer(1)
            _cast_state["l1_done"] = True
        o = 0
        actb = lambda j, n: A16(j * 1024, n)
        o = 22 * 1024
        UW = (TN // Ls if Ls < TN else 1) * 2 + TN
        nsq_t = max(1, TN // Ls) if Ls < TN else 1
        fbufs = []
        for _pb in range(2):
            ug_ = A32(o, UW); o += 4 * UW
            uv_ = A32(o, UW); o += 4 * UW
            yg_ = A32(o, TN); o += 4 * TN
            yvv_ = A32(o, TN); o += 4 * TN
            fbufs.append((ug_, uv_, yg_, yvv_))
        assert o <= ARENA_B, o
        for s in range(nseq):
            if first:
                P.memset("pool", ftail[:, l, ss_(s), :, :], 0.0)
            elif job.kind == "s":
                load_tails(ftail[:, l, s, :, :].rearrange("p t c -> p (t c)"), 88,
                           I["cache_ffn_conv"][l, s].rearrange("t (c p) -> (t c) p", p=128))
        for ft in range(NTT):
            nsub = TN // 128
            rms_to_hT(lambda i: x_tm[:, ft * nsub + i, :], nsub, GF, l)
            rr = Ring([ps[0][:], ps[1][:], ps[2][:], ps[3][:]])
            ranges = tok_ranges(job, ft * TN, TN)
            for jg in range(6):
                nj = 4 if jg < 5 else 2
                wg = WS.get(("ug", l, ft, jg))
                wv = WS.get(("uv", l, ft, jg))
                for jj in range(nj):
                    j = jg * 4 + jj
                    ug, uv, yg, yv_ = fbufs[j % 2]
                    for (wsel, ub, yb, cix) in ((wg, ug, yg, j), (wv, uv, yv_, 22 + j)):
                        pt = rr.get()[:, 0:TN]
                        for kc in range(8):
                            P.mm(pt, wsel[:, kc * nj * 128 + jj * 128: kc * nj * 128 + (jj + 1) * 128], hT[:, kc, 0:TN],
                                 start=(kc == 0), stop=(kc == 7))
                        for ri, (s, ts_, co, n) in enumerate(ranges):
                            ubase = ri * (2 + n)
                            P.copy("pool", ub[:, ubase:ubase + 2], ftail[:, l, ss_(s), :, cix])
                            P.copy("act", ub[:, ubase + 2:ubase + 2 + n], pt[:, co:co + n])
                            yy = yb[:, co:co + n]
                            P.ts("dve", yy, ub[:, ubase + 2:ubase + 2 + n], pr(FW + 2 * 44 + cix), pr(FB + cix), ALU.mult, ALU.add)
                            P.stt(yy, ub[:, ubase + 1:ubase + 1 + n], pr(FW + 44 + cix), yy, ALU.mult, ALU.add)
                            P.stt(yy, ub[:, ubase:ubase + n], pr(FW + cix), yy, ALU.mult, ALU.add)
                            P.copy("pool", ftail[:, l, ss_(s), :, cix], ub[:, ubase + n:ubase + n + 2])
                    P.act(yg[:, 0:TN], yg[:, 0:TN], AF.Silu)
                    P.tt("pool", actb(j, TN), yg[:, 0:TN], yv_[:, 0:TN], ALU.mult)
            for jg in range(6):
                nj = 4 if jg < 5 else 2
                wd = WS.get(("dn", l, ft, jg))
                for jj in range(nj):
                    j = jg * 4 + jj
                    for i in range(nsub):
                        for half in range(2):
                            P.mm(ps[i * 2 + half][:], actb(j, TN)[:, i * 128:(i + 1) * 128],
                                 wd[:, jj * 1024 + half * 512: jj * 1024 + (half + 1) * 512], start=(j == 0), stop=(j == 21))
            for i in range(nsub):
                for half in range(2):
                    xs_ = x_tm[:, ft * nsub + i, half * 512:(half + 1) * 512]
                    P.tt("dve", xs_, xs_, ps[i * 2 + half][:], ALU.add)
        if last:
            for s in range(nseq):
                store_tails(ftail[:, l, ss_(s), :, :].rearrange("p t c -> p (t c)"), 88,
                            OP("ff")[l, bb(s)].rearrange("t (c p) -> (t c) p", p=128), "tl_out")
        ck(9)

    def final_norm(job):
        NT = job.T // 128
        P.memset("pool", nrm_s[:, 0:NT], 0.0)
        for i in range(NT):
            P.act(junk[:], x_tm[:, i, :], AF.Square, accum_out=nrm_s[:, i:i + 1])
        P.ts("dve", nrm_s[:, 8:8 + NT], nrm_s[:, 0:NT], 1.0 / D, EPS, ALU.mult, ALU.add)
        P.act(nrm_s[:, 8:8 + NT], nrm_s[:, 8:8 + NT], AF.Sqrt)
        P.recip(nrm_s[:, 8:8 + NT], nrm_s[:, 8:8 + NT])
        for i in range(NT):
            yb = (A32(0, D), A32(4096, D))[i % 2]
            P.stt(yb, x_tm[:, i, :], nrm_s[:, 8 + i:9 + i], gfin[:], ALU.mult, ALU.mult)
            for (s, ts_, co, n) in tok_ranges(job, i * 128, 128):
                if job.kind == "p":
                    t_abs = job.chunk * TCH + ts_
                    P.store("act", O["y_p"][job.b, t_abs:t_abs + n, :], yb[co:co + n, :], "y%d" % (i % 2))
                else:
                    P.store("act", O["y_s"][s, ts_:ts_ + n, :], yb[co:co + n, :], "y%d" % (i % 2))

    ck_jobs = jobs if STOP >= 99 else ([jobs[int(os.environ.get('KJOB', '0'))]] if STOP < 50 else jobs[:STOP - 50])
    if os.environ.get('KPOISON', '0') != '0':
        pv = float('nan') if os.environ['KPOISON'] == 'nan' else float(os.environ['KPOISON'])
        which = os.environ.get('KPOISON_WHICH', 'all').split(',')
        cand = {"x_tm": x_tm[:, :, :], "hT": hT[:, :, :], "mixT": mixT[:, :, :], "KTc": KTc[:, :], "Vc": Vc[:, :], "KTn": KTn[:, :],
                "Vn": Vn[:, :], "arena": ar32[:, :], "qk2": qk2[:, :], "junk": junk[:, :], "xsb": xsb[:, :], "smallA": smallA[:, :, :],
                "Sst": Sst[:, :, :, :, :], "ctail": ctail[:, :, :, :, :], "ptail": ptail[:, :, :, :, :], "ftail": ftail[:, :, :, :, :],
                "nrm_s": nrm_s[:, :], "tstg": tstg[:, :], "wsl0": wsl[0][:, :], "wsl1": wsl[1][:, :], "wsl2": wsl[2][:, :], "wsl3": wsl[3][:, :]}
        for nm, ap_ in cand.items():
            if 'all' in which or nm in which:
                P.memset("pool", ap_, pv)
        if 'all' in which or 'psum' in which:
            for b_ in range(8):
                P.memset("dve", ps[b_][:, :], pv)
    try:
        ck(0)
        for job in ck_jobs:
            NT = job.T // 128
            for i in range(NT):
                if job.kind == "p":
                    t0 = job.chunk * TCH + i * 128
                    P.load("sp", x_tm[:, i, :], I["x_prompt"][job.b, t0:t0 + 128, :], "x")
                else:
                    P.load("sp", x_tm[0:64, 0, :], I["x_sample"][0], "x")
                    P.load("sp", x_tm[64:128, 0, :], I["x_sample"][1], "x")
            for l in range(2):
                layer_step(job, l)
            final_norm(job)
    except _Stop:
        if os.environ.get('KDUMP', '0') == '1':
            for i in range(8):
                P.store("act", O["y_p"][0, i * 128:(i + 1) * 128, :], x_tm[:, i, :], "ydump")

    keys = [k for k in P.dma_issued if k.startswith(("y", "st_out", "tl_out", "ktok", "vtok"))]
    P.emit(final_wait_keys=keys)
    es.close()
    return nc, P


_CACHE = {}


def kernel(**inputs):
    if "nc" not in _CACHE:
        _CACHE["nc"] = build_program()[0]
    nc = _CACHE["nc"]
    n = 8
    f = lambda a: np.ascontiguousarray(np.asarray(a, dtype=np.float32))
    in_maps = []
    for c in range(n):
        bs = slice(2 * c, 2 * c + 2)
        m = {}
        m["x_prompt"] = f(inputs["x_prompt"][bs])
        m["x_sample"] = f(inputs["x_sample"][bs])
        m["state_delta"] = f(inputs["state_delta"][:, bs])
        m["cache_qkv_conv"] = f(inputs["cache_qkv_conv"][:, bs])
        m["cache_k"] = f(inputs["cache_k"][:, bs]).reshape(2, 2, 2048, 512)
        m["cache_v"] = f(inputs["cache_v"][:, bs]).reshape(2, 2, 2048, 512)
        m["cache_pool"] = f(inputs["cache_pool"][:, bs])
        m["cache_ffn_conv"] = f(inputs["cache_ffn_conv"][:, bs])
        for k in ("norm_mix", "w_in", "a_conv_w", "a_conv_b", "a_log", "a_dt_bias", "a_norm", "b_norm", "c_w",
                  "c_scale", "w_out", "norm_ffn", "ffn_up", "ffn_conv_w", "ffn_conv_b", "ffn_down", "norm_final"):
            m[k] = f(inputs[k])
        m["b_lambda"] = f(inputs["b_lambda"]).reshape(2, 256)
        in_maps.append(m)
    res = run_bass_kernel_spmd(nc, in_maps, core_ids=list(range(n)))
    R = res.results
    cat0 = lambda k: np.concatenate([r[k] for r in R], axis=0)
    cat1 = lambda k: np.concatenate([r[k] for r in R], axis=1)
    outs = {
        "y_p": cat0("y_p"), "y_s": cat0("y_s"),
        "st_p": cat1("st_p"), "st_s": cat1("st_s"),
        "cv_p": cat1("cv_p"), "cv_s": cat1("cv_s"),
        "k_p": cat1("k_p").reshape(2, 16, 2048, 4, 128), "k_s": cat1("k_s").reshape(2, 16, 64, 4, 128),
        "v_p": cat1("v_p").reshape(2, 16, 2048, 4, 128), "v_s": cat1("v_s").reshape(2, 16, 64, 4, 128),
        "pl_p": cat1("pl_p"), "pl_s": cat1("pl_s"),
        "ff_p": cat1("ff_p"), "ff_s": cat1("ff_s"),
    }
    return tuple(np.ascontiguousarray(outs[k], dtype=np.float32) for k in OUT_ORDER)
```

```python
import math
import os
from contextlib import ExitStack
import numpy as np
import concourse.bass as bass
import concourse.mybir as mybir
from concourse.bass_utils import run_bass_kernel_spmd

F32 = mybir.dt.float32
BF16 = mybir.dt.bfloat16
ALU = mybir.AluOpType
AF = mybir.ActivationFunctionType
ESZ = {F32: 4, BF16: 2}
ENGS = ("pe", "act", "dve", "pool", "sp")

D = 1024
NKC = 8
FF = 2816
NJ = 22
EPS = 1e-6
TCH = 1024
C_AQ, C_AK, C_AV, C_AZ, C_AB, C_BQ, C_BK, C_BV, C_CX = 0, 512, 1024, 1536, 2048, 2056, 2568, 3080, 3592
GM, CW, CB, AN, BN, CS, GF, FB, FW, NPRM = 0, 8, 56, 68, 69, 70, 74, 82, 126, 258


PSNAMES = {"ps%d" % i for i in range(8)}


def region(ap):
    if ap.name in PSNAMES:
        return (ap.name, 0, 128, 0, 2048)
    t = ap.tensor
    es = ESZ.get(ap.dtype, 4)
    pstep = 1
    for s in t.shape[1:]:
        pstep *= s
    off = ap.offset
    p_lo = off // pstep
    f_lo = off % pstep
    p_cnt = 1
    f_ext = 0
    for i, (st, cnt) in enumerate(ap.ap):
        if i == 0 and st == pstep:
            p_cnt = cnt
        elif i == 0 and st == 0:
            p_cnt = 1
        else:
            f_ext += abs(st) * (cnt - 1)
    return (ap.name, p_lo, p_lo + p_cnt, f_lo * es, (f_lo + f_ext + 1) * es)


def overlap(a, b):
    return a[1] < b[2] and b[1] < a[2] and a[3] < b[4] and b[3] < a[4]


def covers(a, b):
    return a[1] <= b[1] and a[2] >= b[2] and a[3] <= b[3] and a[4] >= b[4]


class Op:
    __slots__ = ("eng", "fn", "eidx", "waits", "signal", "dma_key", "dma_waits")

    def __init__(self, eng, fn, eidx):
        self.eng = eng
        self.fn = fn
        self.eidx = eidx
        self.waits = {}
        self.dma_waits = {}
        self.signal = False
        self.dma_key = None


class Prog:
    def __init__(self, nc):
        self.nc = nc
        self.estream = {e: [] for e in ENGS}
        self.acc = {}
        self.dma_issued = {}
        self.known = {e: {f: -1 for f in ENGS} for e in ENGS}
        self.known_dma = {e: {} for e in ENGS}
        self.nops = 0

    def _dep_on(self, op, dep):
        if dep is op:
            return
        if dep.dma_key is not None:
            k = dep.dma_key
            cnt = self.dma_issued[k]
            if self.known_dma[op.eng].get(k, 0) >= cnt:
                return
            op.dma_waits[k] = max(op.dma_waits.get(k, 0), cnt)
            return
        f = dep.eng
        if f == "pe" and op.eng == "pe":
            return
        if self.known[op.eng][f] >= dep.eidx:
            return
        op.waits[f] = max(op.waits.get(f, -1), dep.eidx)

    def add(self, eng, fn, reads=(), writes=(), dma_key=None):
        op = Op(eng, fn, len(self.estream[eng]))
        op.dma_key = dma_key
        rregs = [region(a) for a in reads if a is not None]
        wregs = [region(a) for a in writes if a is not None]
        wregs = wregs + [r for r in rregs if r[0] in PSNAMES]
        rregs = [r for r in rregs if r[0] not in PSNAMES]
        for r in rregs:
            lst = self.acc.get(r[0])
            if lst:
                for (reg, dep, isw) in lst:
                    if isw and overlap(reg, r):
                        self._dep_on(op, dep)
        for w in wregs:
            lst = self.acc.get(w[0])
            if lst:
                for (reg, dep, isw) in lst:
                    if overlap(reg, w):
                        self._dep_on(op, dep)
        for f, ei in op.waits.items():
            self.estream[f][ei].signal = True
            if ei > self.known[eng][f]:
                self.known[eng][f] = ei
        for k, c in op.dma_waits.items():
            self.known_dma[eng][k] = c
        if dma_key is not None:
            self.dma_issued[dma_key] = self.dma_issued.get(dma_key, 0) + 1
        for w in wregs:
            lst = self.acc.setdefault(w[0], [])
            lst[:] = [e for e in lst if not covers(w, e[0])]
            lst.append((w, op, True))
        for r in rregs:
            lst = self.acc.setdefault(r[0], [])
            if dma_key is None:
                lst[:] = [e for e in lst if not ((not e[2]) and e[1].eng == eng and e[1].dma_key is None
                                                 and covers(r, e[0]))]
            lst.append((r, op, False))
        self.estream[eng].append(op)
        self.nops += 1
        return op

    def emit(self, final_wait_keys=()):
        nc = self.nc
        with ExitStack() as es:
            sems = {e: es.enter_context(nc.semaphore("s_" + e)) for e in ENGS if e != "sp"}
            dsems = {k: es.enter_context(nc.semaphore("d_%s" % str(k))) for k in self.dma_issued}
            block = es.enter_context(nc.Block())
            sigcount = {}
            for e in ENGS:
                c = 0
                for op in self.estream[e]:
                    if op.signal and op.dma_key is None:
                        c += 1
                    sigcount[(e, op.eidx)] = c

            def run(e, eng):
                for op in self.estream[e]:
                    for f, ei in op.waits.items():
                        eng.wait_ge(sems[f], sigcount[(f, ei)])
                    for k, c in op.dma_waits.items():
                        eng.wait_ge(dsems[k], 16 * c)
                    ins = op.fn(eng)
                    if op.dma_key is not None:
                        ins.then_inc(dsems[op.dma_key], 16)
                    elif op.signal:
                        ins.then_inc(sems[e], 1)
                if e == "sp":
                    for k in final_wait_keys:
                        eng.wait_ge(dsems[k], 16 * self.dma_issued[k])

            @block.tensor
            def _(eng):
                run("pe", eng)

            @block.scalar
            def _(eng):
                run("act", eng)

            @block.vector
            def _(eng):
                run("dve", eng)

            @block.gpsimd
            def _(eng):
                run("pool", eng)

            @block.sync
            def _(eng):
                run("sp", eng)

    def load(self, q, out_sb, in_dram, key):
        return self.add(q, lambda e: e.dma_start(out=out_sb, in_=in_dram), writes=[out_sb], dma_key=key)

    def store(self, q, out_dram, in_sb, key):
        return self.add(q, lambda e: e.dma_start(out=out_dram, in_=in_sb), reads=[in_sb], dma_key=key)

    def mm(self, out, lhsT, rhs, start=True, stop=True):
        return self.add("pe", lambda e: e.matmul(out, lhsT, rhs, start=start, stop=stop),
                        reads=[lhsT, rhs] + ([] if start else [out]), writes=[out])

    def act(self, out, in_, func, bias=None, scale=None, accum_out=None):
        kw = {}
        rd = [in_]
        if bias is not None:
            kw["bias"] = bias
            if not isinstance(bias, (int, float)):
                rd.append(bias)
        if scale is not None:
            kw["scale"] = scale
            if not isinstance(scale, (int, float)):
                rd.append(scale)
        wr = [out]
        if accum_out is not None:
            kw["accum_out"] = accum_out
            wr.append(accum_out)
        return self.add("act", lambda e: e.activation(out, in_, func, **kw), reads=rd, writes=wr)

    def ts(self, eng, out, in0, s1, s2, op0, op1=None):
        rd = [in0]
        if not isinstance(s1, (int, float)):
            rd.append(s1)
        if s2 is not None and not isinstance(s2, (int, float)):
            rd.append(s2)
        if op1 is None:
            return self.add(eng, lambda e: e.tensor_scalar(out, in0, s1, None, op0), reads=rd, writes=[out])
        return self.add(eng, lambda e: e.tensor_scalar(out, in0, s1, s2, op0, op1), reads=rd, writes=[out])

    def tt(self, eng, out, in0, in1, op):
        return self.add(eng, lambda e: e.tensor_tensor(out, in0, in1, op), reads=[in0, in1], writes=[out])

    def stt(self, out, in0, scalar, in1, op0, op1):
        rd = [in0, in1]
        if not isinstance(scalar, (int, float)):
            rd.append(scalar)
        return self.add("dve", lambda e: e.scalar_tensor_tensor(out, in0, scalar, in1, op0, op1),
                        reads=rd, writes=[out])

    def copy(self, eng, out, in_):
        if eng == "act":
            return self.add("act", lambda e: e.copy(out, in_), reads=[in_], writes=[out])
        return self.add(eng, lambda e: e.tensor_copy(out, in_), reads=[in_], writes=[out])

    def memset(self, eng, out, val):
        return self.add(eng, lambda e: e.memset(out, val), writes=[out])

    def recip(self, out, in_):
        return self.add("dve", lambda e: e.reciprocal(out, in_), reads=[in_], writes=[out])


IN_SHAPES = {
    "x_prompt": [2, 2048, D], "x_sample": [2, 64, D],
    "state_delta": [2, 2, 4, 128, 128], "cache_qkv_conv": [2, 2, 3, 1536],
    "cache_k": [2, 2, 2048, 512], "cache_v": [2, 2, 2048, 512],
    "cache_pool": [2, 2, 15, 512], "cache_ffn_conv": [2, 2, 2, 5632],
    "norm_mix": [2, D], "w_in": [2, D, 4104], "a_conv_w": [2, 4, 1536], "a_conv_b": [2, 1536],
    "a_log": [2, 4], "a_dt_bias": [2, 4], "a_norm": [2, 128], "b_lambda": [2, 256], "b_norm": [2, 128],
    "c_w": [2, 4, 128, 128], "c_scale": [2, 512], "w_out": [2, 1536, D], "norm_ffn": [2, D],
    "ffn_up": [2, D, 5632], "ffn_conv_w": [2, 3, 5632], "ffn_conv_b": [2, 5632], "ffn_down": [2, FF, D],
    "norm_final": [D],
}
OUT_SHAPES = {
    "y_p": [2, 2048, D], "y_s": [2, 64, D],
    "st_p": [2, 2, 4, 128, 128], "st_s": [2, 2, 4, 128, 128],
    "cv_p": [2, 2, 3, 1536], "cv_s": [2, 2, 3, 1536],
    "k_p": [2, 2, 2048, 512], "k_s": [2, 2, 64, 512],
    "v_p": [2, 2, 2048, 512], "v_s": [2, 2, 64, 512],
    "pl_p": [2, 2, 15, 512], "pl_s": [2, 2, 15, 512],
    "ff_p": [2, 2, 2, 5632], "ff_s": [2, 2, 2, 5632],
}
OUT_ORDER = ["y_p", "y_s", "st_p", "st_s", "cv_p", "cv_s", "k_p", "k_s", "v_p", "v_s", "pl_p", "pl_s", "ff_p", "ff_s"]


class _Stop(Exception):
    pass


def build_program():
    STOP = int(os.environ.get('KSTOP', '99'))

    def ck(n):
        if STOP == n:
            raise _Stop()

    nc = bass.Bass("TRN2", target_bir_lowering=False)
    I = {k: nc.dram_tensor(k, s, F32, kind="ExternalInput").ap() for k, s in IN_SHAPES.items()}
    O = {k: nc.dram_tensor(k, s, F32, kind="ExternalOutput").ap() for k, s in OUT_SHAPES.items()}
    P = Prog(nc)
    es = ExitStack()
    WSHAPE = {"w_in": (D, 4104), "w_out": (1536, D), "ffn_up": (D, 2 * FF), "ffn_down": (FF, D)}
    SCR = {k: nc.dram_tensor("scr_" + k, [2, r, c], BF16, kind="Internal").ap() for k, (r, c) in WSHAPE.items()}
    sb = lambda n, s, d: es.enter_context(nc.sbuf_tensor(n, s, d))
    x_tm = sb("x_tm", [128, 8, D], F32)
    hT = sb("hT", [128, NKC, TCH], BF16)
    mixT = sb("mixT", [128, 4, TCH], BF16)
    KTc = sb("KTc", [128, 8192], BF16)
    Vc = sb("Vc", [128, 8192], BF16)
    KTn = sb("KTn", [128, 4096], BF16)
    Vn = sb("Vn", [128, 4096], BF16)
    KTn32 = KTn.bitcast(F32)
    Vn32 = Vn.bitcast(F32)
    NSLOT = 4
    wsl = [sb("wsl%d" % i, [128, 4096], BF16) for i in range(NSLOT)]
    ARENA_B = 39936
    ar32 = sb("arena", [128, ARENA_B // 4], F32)
    ar16 = ar32.bitcast(BF16)
    prmT = sb("prmT", [128, 2, NPRM], F32)
    ident_b = sb("ident_b", [128, 128], BF16)
    ident_f = sb("ident_f", [128, 128], F32)
    ones_f = sb("ones_f", [128, 128], F32)
    ones_b = sb("ones_b", [128, 128], BF16)
    TRI = sb("TRI", [128, 128], F32)
    MASKL = sb("MASKL", [128, 128], F32)
    BLK = sb("BLK", [128, 128], F32)
    CHK = sb("CHK", [128, 2, 128], F32)
    chkcol = sb("chkcol", [128, 2], F32)
    gfin = sb("gfin", [128, D], F32)
    cwb = sb("cwb", [128, 2, 4, 128], BF16)
    Sst = sb("Sst", [128, 2, 2, 4, 128], F32)
    Sst16 = Sst.bitcast(BF16)
    ctail = sb("ctail", [128, 2, 2, 3, 12], F32)
    ptail = sb("ptail", [128, 2, 2, 15, 4], F32)
    ftail = sb("ftail", [128, 2, 2, 2, 44], F32)
    invcnt = sb("invcnt", [128, 4, 16], F32)
    bcast = sb("bcast", [128, 2, 12], F32)
    bnS = sb("bnS", [128, 2], F32)
    stg = sb("stg", [128, 512], F32)
    smallA = sb("smallA", [128, 8, 40], F32)
    nrm_s = sb("nrm_s", [128, 16], F32)
    junk = sb("junk", [128, D], BF16)
    xsb = sb("xsb", [128, D], BF16)
    tstg = sb("tstg", [128, 128], F32)
    ystg = None
    ps = [es.enter_context(nc.psum_tensor("ps%d" % i, [128, 512], F32)) for i in range(8)]

    class Ring:
        def __init__(self, tiles):
            self.t = tiles
            self.i = 0

        def get(self):
            r = self.t[self.i % len(self.t)]
            self.i += 1
            return r

    def small_tiles(banks, ncols=128):
        return [ps[b][:, c * ncols:(c + 1) * ncols] for c in range(512 // ncols) for b in banks]

    def A32(off, n):
        assert off % 4 == 0 and off + 4 * n <= ARENA_B, (off, n)
        return ar32[:, off // 4: off // 4 + n]

    def A16(off, n):
        assert off % 2 == 0 and off + 2 * n <= ARENA_B, (off, n)
        return ar16[:, off // 2: off // 2 + n]

    def affsel(out, pattern, cmp, fill, base, cm):
        P.add("pool", lambda e: e.affine_select(out, out, pattern, cmp, fill, base=base, channel_multiplier=cm),
              reads=[out], writes=[out])

    P.memset("pool", ident_f[:], 0.0)
    affsel(ident_f[:], [[-1, 128]], ALU.not_equal, 1.0, 0, 1)
    P.copy("dve", ident_b[:], ident_f[:])
    P.memset("pool", ones_f[:], 1.0)
    P.memset("pool", ones_b[:], 1.0)
    P.memset("pool", TRI[:], 1.0)
    affsel(TRI[:], [[1, 128]], ALU.is_ge, 0.0, 0, -1)
    P.memset("pool", TRI[0:64, 64:128], 0.0)
    P.memset("pool", MASKL[:], 1.0)
    affsel(MASKL[:], [[-1, 128]], ALU.is_gt, 0.0, 0, 1)
    P.memset("pool", MASKL[64:128, 0:64], 0.0)
    P.memset("pool", BLK[:], 0.0)
    P.memset("pool", BLK[0:64, 0:64], 1.0)
    P.memset("pool", BLK[64:128, 64:128], 1.0)
    P.memset("pool", CHK[:], 0.0)
    P.memset("pool", CHK[0:64, 0, :], 1.0)
    P.memset("pool", CHK[64:128, 1, :], 1.0)
    P.memset("pool", chkcol[:], 0.0)
    P.memset("pool", chkcol[0:64, 0:1], 1.0)
    P.memset("pool", chkcol[64:128, 1:2], 1.0)
    for g, win in enumerate((2, 4, 8, 16)):
        P.memset("pool", invcnt[:, g, :], 1.0 / win)
        for t in range(win - 1):
            P.memset("pool", invcnt[:, g, t:t + 1], 1.0 / (t + 1))

    def bcast_row(dst, row_ap, n):
        pt = ps[7][:, 0:n]
        P.mm(pt, ones_f[0:1, :], row_ap)
        P.copy("dve", dst, pt)

    P.load("sp", stg[0:1, 0:512], I["norm_final"][0:512].rearrange("(o n) -> o n", o=1), "stg")
    bcast_row(gfin[:, 0:512], stg[0:1, 0:512], 512)
    P.load("sp", stg[0:1, 0:512], I["norm_final"][512:1024].rearrange("(o n) -> o n", o=1), "stg")
    bcast_row(gfin[:, 512:1024], stg[0:1, 0:512], 512)

    for l in range(2):
        rows = []
        rows.append((I["norm_mix"][l].rearrange("(n p) -> n p", p=128), 8))
        rows.append((I["a_conv_w"][l].rearrange("t (c p) -> (t c) p", p=128), 48))
        rows.append((I["a_conv_b"][l].rearrange("(n p) -> n p", p=128), 12))
        rows.append((I["a_norm"][l].rearrange("(n p) -> n p", p=128), 1))
        rows.append((I["b_norm"][l].rearrange("(n p) -> n p", p=128), 1))
        rows.append((I["c_scale"][l].rearrange("(n p) -> n p", p=128), 4))
        rows.append((I["norm_ffn"][l].rearrange("(n p) -> n p", p=128), 8))
        rows.append((I["ffn_conv_b"][l].rearrange("(n p) -> n p", p=128), 44))
        r0 = 0
        for ap, n in rows:
            P.load("sp", stg[r0:r0 + n, 0:128], ap, "stg")
            r0 += n
        assert r0 == 126
        pt = ps[6][:, 0:126]
        P.mm(pt, stg[0:126, 0:128], ident_f[0:126, 0:126])
        P.copy("dve", prmT[:, l, 0:126], pt)
        fw = I["ffn_conv_w"][l].rearrange("t (c p) -> (t c) p", p=128)
        P.load("sp", stg[0:128, 128:256], fw[0:128, :], "stg")
        pt = ps[6][:, 128:256]
        P.mm(pt, stg[0:128, 128:256], ident_f[:, :])
        P.copy("dve", prmT[:, l, 126:254], pt)
        P.load("sp", stg[0:4, 256:384], fw[128:132, :], "stg")
        pt = ps[6][:, 256:260]
        P.mm(pt, stg[0:4, 256:384], ident_f[0:4, 0:4])
        P.copy("dve", prmT[:, l, 254:258], pt)
        P.load("sp", stg[0:1, 384:388], I["a_log"][l].rearrange("(o n) -> o n", o=1), "stg")
        P.load("sp", stg[0:1, 388:392], I["a_dt_bias"][l].rearrange("(o n) -> o n", o=1), "stg")
        P.load("sp", stg[0:1, 128:384], I["b_lambda"][l].rearrange("(o n) -> o n", o=1), "stg2")
        lam_init = 0.8 - 0.6 * math.exp(-0.3 * l)
        P.tt("dve", stg[0:1, 400:464], stg[0:1, 128:192], stg[0:1, 192:256], ALU.mult)
        P.tt("dve", stg[0:1, 464:512], stg[0:1, 256:304], stg[0:1, 320:368], ALU.mult)
        P.tt("dve", stg[0:1, 128:144], stg[0:1, 304:320], stg[0:1, 368:384], ALU.mult)
        P.add("dve", lambda e: e.reduce_sum(stg[0:1, 392:393], stg[0:1, 400:464], mybir.AxisListType.X),
              reads=[stg[0:1, 400:464]], writes=[stg[0:1, 392:393]])
        P.add("dve", lambda e: e.reduce_sum(stg[0:1, 393:394], stg[0:1, 464:512], mybir.AxisListType.X),
              reads=[stg[0:1, 464:512]], writes=[stg[0:1, 393:394]])
        P.add("dve", lambda e: e.reduce_sum(stg[0:1, 394:395], stg[0:1, 128:144], mybir.AxisListType.X),
              reads=[stg[0:1, 128:144]], writes=[stg[0:1, 394:395]])
        P.tt("dve", stg[0:1, 393:394], stg[0:1, 393:394], stg[0:1, 394:395], ALU.add)
        P.act(stg[0:1, 392:394], stg[0:1, 392:394], AF.Exp)
        P.tt("dve", stg[0:1, 395:396], stg[0:1, 393:394], stg[0:1, 392:393], ALU.subtract)
        P.ts("dve", stg[0:1, 395:396], stg[0:1, 395:396], -lam_init, None, ALU.add)
        P.act(stg[0:1, 384:388], stg[0:1, 384:388], AF.Exp)
        P.ts("dve", stg[0:1, 384:388], stg[0:1, 384:388], -1.0, None, ALU.mult)
        bcast_row(bcast[:, l, 0:12], stg[0:1, 384:396], 12)
        P.ts("dve", bnS[:, l:l + 1], prmT[:, l, BN:BN + 1], 1.0 - lam_init, None, ALU.mult)
        for g in range(4):
            P.load("pool", cwb[:, l, g, :], I["c_w"][l, g], "cw")
    NEGLAM = 11
    CASTN = {}

    def cast_layer(l):
        for k in ("w_in", "w_out", "ffn_up", "ffn_down"):
            rows = WSHAPE[k][0]
            key = "cast_%s_%d" % (k, l)
            r0 = 0
            while r0 < rows:
                n = min(256, rows - r0)
                P.add("pool", (lambda e, k=k, l=l, r0=r0, n=n: e.dma_start(out=SCR[k][l, r0:r0 + n, :], in_=I[k][l, r0:r0 + n, :])),
                      dma_key=key)
                r0 += n
            CASTN[key] = P.dma_issued[key]

    cast_layer(0)
    _cast_state = {"l1_done": False}

    class WStream:
        def __init__(self):
            self.reqs = []
            self.issued = 0
            self.cur = 0

        def plan(self, key, pieces):
            self.reqs.append((key, pieces))

        def _issue(self, idx):
            key, pieces = self.reqs[idx]
            slot = wsl[idx % NSLOT]
            off = 0
            for (wname, l, r0, c0, nk, ncols) in pieces:
                src = SCR[wname][l, r0:r0 + nk * 128, c0:c0 + ncols].rearrange("(k p) c -> p k c", p=128)
                base = slot[:, off:off + 1]
                dst = bass.AP(tensor=base.tensor, offset=base.offset, ap=[[4096, 128], [ncols, nk], [1, ncols]])
                op = P.add("sp", (lambda e, dst=dst, src=src: e.dma_start(out=dst, in_=src)),
                           writes=[slot[:, off:off + nk * ncols]], dma_key="w%d" % (idx % NSLOT))
                ck_ = "cast_%s_%d" % (wname, l)
                if P.known_dma["sp"].get(ck_, 0) < CASTN[ck_]:
                    op.dma_waits[ck_] = CASTN[ck_]
                    P.known_dma["sp"][ck_] = CASTN[ck_]
                off += nk * ncols
            assert off <= 4096

        def get(self, key, ahead=NSLOT - 2):
            assert self.reqs[self.cur][0] == key, (self.reqs[self.cur][0], key)
            while self.issued < min(len(self.reqs), self.cur + ahead + 1):
                self._issue(self.issued)
                self.issued += 1
            slot = wsl[self.cur % NSLOT]
            self.cur += 1
            return slot

    WS = WStream()

    def plan_layer(l, T):
        WS.plan(("ab", l), [("w_in", l, 0, C_AB, 8, 8)])
        for h in range(4):
            WS.plan(("A", l, h), [("w_in", l, 0, c + h * 128, 8, 128) for c in (C_AQ, C_AK, C_AV, C_AZ)])
        WS.plan(("wo", l, 0), [("w_out", l, 0, 0, 4, 1024)])
        WS.plan(("bk", l), [("w_in", l, 0, C_BK, 8, 512)])
        WS.plan(("bv", l), [("w_in", l, 0, C_BV, 8, 512)])
        for h in range(4):
            WS.plan(("B", l, h), [("w_in", l, 0, c + h * 128, 8, 128) for c in (C_BQ, C_BK)])
        WS.plan(("wo", l, 1), [("w_out", l, 512, 0, 4, 1024)])
        WS.plan(("cx", l), [("w_in", l, 0, C_CX, 8, 512)])
        WS.plan(("wo", l, 2), [("w_out", l, 1024, 0, 4, 1024)])
        for ft in range(max(1, T // 512)):
            for jg in range(6):
                nj = 4 if jg < 5 else 2
                WS.plan(("ug", l, ft, jg), [("ffn_up", l, 0, jg * 512, 8, nj * 128)])
                WS.plan(("uv", l, ft, jg), [("ffn_up", l, 0, FF + jg * 512, 8, nj * 128)])
            for jg in range(6):
                nj = 4 if jg < 5 else 2
                WS.plan(("dn", l, ft, jg), [("ffn_down", l, jg * 512, 0, nj, 1024)])

    class Job:
        pass

    jobs = []
    for b in range(2):
        for c in range(2):
            j = Job()
            j.kind, j.b, j.chunk, j.nseq, j.Ls, j.T = "p", b, c, 1, TCH, TCH
            jobs.append(j)
    j = Job()
    j.kind, j.b, j.chunk, j.nseq, j.Ls, j.T = "s", 0, 0, 2, 64, 128
    jobs.append(j)
    for j in jobs:
        for l in range(2):
            plan_layer(l, j.T)

    def rms_to_hT(xtile_fn, ntiles, gcol, l):
        P.memset("pool", nrm_s[:, 0:ntiles], 0.0)
        for i in range(ntiles):
            P.act(junk[:], xtile_fn(i), AF.Square, accum_out=nrm_s[:, i:i + 1])
        P.ts("dve", nrm_s[:, 8:8 + ntiles], nrm_s[:, 0:ntiles], 1.0 / D, EPS, ALU.mult, ALU.add)
        P.act(nrm_s[:, 8:8 + ntiles], nrm_s[:, 8:8 + ntiles], AF.Sqrt)
        P.recip(nrm_s[:, 8:8 + ntiles], nrm_s[:, 8:8 + ntiles])
        KD = os.environ.get('KDBG', 'd')
        if KD == 'a':
            return
        for i in range(ntiles):
            P.act(xsb[:], xtile_fn(i), AF.Identity, scale=nrm_s[:, 8 + i:9 + i])
            if KD == 'b':
                continue
            for half in range(2):
                bank = ps[half]
                for q in range(4):
                    kc = half * 4 + q
                    P.mm(bank[:, q * 128:(q + 1) * 128], xsb[:, kc * 128:(kc + 1) * 128], ident_b[:])
                if KD == 'c':
                    continue
                for q in range(4):
                    kc = half * 4 + q
                    eng = "dve" if q % 2 == 0 else "pool"
                    if KD == 'e':
                        eng = "dve"
                    if KD == 'f':
                        eng = "pool"
                    if KD == 'g':
                        P.copy("dve", hT[:, kc, i * 128:(i + 1) * 128], bank[:, q * 128:(q + 1) * 128])
                        continue
                    if KD == 'h':
                        P.ts("dve", junk[:, kc * 128:(kc + 1) * 128], bank[:, q * 128:(q + 1) * 128],
                             prmT[:, l, gcol + kc: gcol + kc + 1], None, ALU.mult)
                        continue
                    if eng == "pool":
                        P.act(hT[:, kc, i * 128:(i + 1) * 128], bank[:, q * 128:(q + 1) * 128], AF.Identity,
                              scale=prmT[:, l, gcol + kc: gcol + kc + 1])
                    else:
                        P.ts("dve", hT[:, kc, i * 128:(i + 1) * 128], bank[:, q * 128:(q + 1) * 128],
                             prmT[:, l, gcol + kc: gcol + kc + 1], None, ALU.mult)

    def rsqrt_ps(dst32, src_ps, scale, n):
        P.ts("dve", dst32, src_ps, scale, EPS, ALU.mult, ALU.add)
        P.act(dst32, dst32, AF.Ln)
        P.act(dst32, dst32, AF.Exp, scale=-0.5)

    def tok_ranges(job, t0, n):
        out = []
        t = t0
        while t < t0 + n:
            s = t // job.Ls
            e = min(t0 + n, (s + 1) * job.Ls)
            out.append((s, t - s * job.Ls, t - t0, e - t))
            t = e
        return out

    def store_tails(tail_ap2d, ncols, dram2d, key):
        pt = ps[7][0:ncols, 128:256]
        P.mm(pt, tail_ap2d, ident_f[:, :])
        P.copy("dve", tstg[0:ncols, :], pt)
        P.store("act", dram2d, tstg[0:ncols, :], key)

    def load_tails(tail_ap2d, ncols, dram2d):
        P.load("sp", tstg[0:ncols, :], dram2d, "tstg_l")
        pt = ps[7][:, 256:256 + ncols]
        P.mm(pt, tstg[0:ncols, :], ident_f[0:ncols, 0:ncols])
        P.copy("dve", tail_ap2d, pt)

    def layer_step(job, l):
        T, nseq, Ls = job.T, job.nseq, job.Ls
        NT = T // 128
        TN = min(512, T)
        NTT = T // TN
        first = (job.kind == "p" and job.chunk == 0)
        last = (job.kind == "s") or (job.chunk == 1)
        bb = lambda s: (job.b if job.kind == "p" else s)
        ss_ = lambda s: (0 if job.kind == "p" else s)
        OP = (lambda k: O[k + "_p"]) if job.kind == "p" else (lambda k: O[k + "_s"])
        pr = lambda c: prmT[:, l, c:c + 1]

        rms_to_hT(lambda i: x_tm[:, i, :], NT, GM, l)
        ck(1)

        wab = WS.get(("ab", l))
        SM = smallA
        def gates_gen():
            for i in range(NT):
                pt = ps[7][:, 0:8]
                for kc in range(8):
                    P.mm(pt, hT[:, kc, i * 128:(i + 1) * 128], wab[:, kc * 8:(kc + 1) * 8], start=(kc == 0), stop=(kc == 7))
                P.copy("dve", SM[:, i, 0:8], pt)
                yield
                P.tt("dve", SM[:, i, 8:12], SM[:, i, 0:4], bcast[:, l, 4:8], ALU.add)
                P.act(SM[:, i, 8:12], SM[:, i, 8:12], AF.Exp)
                yield
                P.ts("dve", SM[:, i, 8:12], SM[:, i, 8:12], 1.0, None, ALU.add)
                P.act(SM[:, i, 8:12], SM[:, i, 8:12], AF.Ln)
                yield
                P.tt("dve", SM[:, i, 8:12], SM[:, i, 8:12], bcast[:, l, 0:4], ALU.mult)
                P.act(SM[:, i, 12:16], SM[:, i, 4:8], AF.Exp, scale=-1.0)
                yield
                P.ts("dve", SM[:, i, 12:16], SM[:, i, 12:16], 1.0, None, ALU.add)
                P.recip(SM[:, i, 12:16], SM[:, i, 12:16])
                yield
                P.ts("dve", SM[:, i, 16:20], SM[:, i, 12:16], -1.0, None, ALU.mult)
                pg = ps[7][:, 16:32]
                P.mm(pg[:, 0:4], TRI[:], SM[:, i, 8:12])
                P.mm(pg[:, 4:8], BLK[:], SM[:, i, 8:12])
                P.mm(pg[:, 8:12], CHK[:, 0, :], SM[:, i, 8:12])
                P.mm(pg[:, 12:16], CHK[:, 1, :], SM[:, i, 8:12])
                P.copy("dve", SM[:, i, 20:36], pg)
                yield
                P.tt("dve", SM[:, i, 24:28], SM[:, i, 24:28], SM[:, i, 20:24], ALU.subtract)
                P.act(SM[:, i, 24:36], SM[:, i, 24:36], AF.Exp)
                yield
                P.act(SM[:, i, 36:40], SM[:, i, 20:24], AF.Exp)
                yield
                P.tt("dve", SM[:, i, 36:40], SM[:, i, 36:40], SM[:, i, 12:16], ALU.mult)
        ck(2)
        XW = nseq * (3 + Ls)
        o = 0
        xraw = A32(o, XW); o += 4 * ((XW + 31) // 32 * 32)
        ycv = A32(o, T); o += 4 * T
        oT = A32(o, T); o += 4 * T
        rt = [A32(o, 512), A32(o + 2048, 512)]; o += 4096
        f32t = [A32(o + 512 * k, 128) for k in range(21)]; o += 512 * 21
        sqb = A16(o, T); o += 2 * T
        qT = A16(o, T); o += 2 * T
        kT = A16(o, T); o += 2 * T
        vT = A16(o, T); o += 2 * T
        zT = A16(o, T); o += 2 * T
        b16t = [A16(o + 256 * k, 128) for k in range(8)]; o += 256 * 8
        if NT == 1:
            set1_small = tuple(A16(o + 256 * k, 128) for k in range(4)); o += 1024
            sqb_n = A16(o, 128); o += 256
            rt_n = [A32(o, 128), A32(o + 512, 128)]; o += 1024
        assert o <= ARENA_B, o
        xf32 = [KTn32[:, 128 * k:128 * (k + 1)] for k in range(16)] + [Vn32[:, 128 * k:128 * (k + 1)] for k in range(4)]
        xb16 = [Vn[:, 1024 + 128 * k:1024 + 128 * (k + 1)] for k in range(8)]
        tdelta = f32t[8]
        ints = [f32t[0:8], xf32[0:8]]
        outs = [f32t[9:15], f32t[15:21], xf32[8:14], xf32[14:20]]
        b16s = [b16t, xb16]
        smr = Ring(small_tiles([2, 3, 4, 5, 6, 7]))
        bigr = Ring([ps[0][:], ps[1][:]])
        P.memset("pool", tdelta, 0.0)

        for s in range(nseq):
            if first:
                P.memset("pool", Sst[:, l, ss_(s), :, :], 0.0)
                P.memset("pool", ctail[:, l, ss_(s), :, :], 0.0)
            elif job.kind == "s":
                for h in range(4):
                    P.load("sp", Sst[:, l, s, h, :], I["state_delta"][l, s, h], "sst")
                load_tails(ctail[:, l, s, :, :], 36, I["cache_qkv_conv"][l, s].rearrange("t (c p) -> (t c) p", p=128))

        sets_dbg = os.environ.get('KSETS', '2')
        _b0 = Sst16[:, 0, 1, 0, 0:1]
        Sst16_q = bass.AP(tensor=_b0.tensor, offset=_b0.offset, ap=[[4096, 128], [1, 1024]])
        _b1 = Sst16[:, 1, 1, 0, 0:1]
        Sst16_k = bass.AP(tensor=_b1.tensor, offset=_b1.offset, ap=[[4096, 128], [1, 1024]])
        sets = [(qT, kT, vT, zT),
                (Sst16_q[:, 0:T], Sst16_k[:, 0:T], Vn[:, 2048:2048 + T], Vn[:, 3072:3072 + T])]
        if sets_dbg == '1':
            sets = [sets[0], sets[0]]
        elif NT == 1:
            sets = [sets[0], set1_small]

        def zipgen(*gens):
            gens = list(gens)
            while gens:
                for g in list(gens):
                    try:
                        next(g)
                        yield
                    except StopIteration:
                        gens.remove(g)

        def chain(*gs):
            for g in gs:
                yield from g

        def run(g):
            for _ in g:
                pass

        def stageC(h):
            qT_, kT_, vT_, zT_ = sets[h % 2]
            wA = WS.get(("A", l, h))
            for pi, (dst, cidx) in enumerate(((qT_, h), (kT_, 4 + h), (vT_, 8 + h), (zT_, None))):
                for tt_ in range(NTT):
                    pt = bigr.get()[:, 0:TN]
                    for kc in range(8):
                        P.mm(pt, wA[:, (pi * 8 + kc) * 128:(pi * 8 + kc + 1) * 128], hT[:, kc, tt_ * TN:(tt_ + 1) * TN],
                             start=(kc == 0), stop=(kc == 7))
                        if kc % 2 == 1:
                            yield
                    if cidx is None:
                        P.act(zT_[:, tt_ * TN:(tt_ + 1) * TN], pt, AF.Silu)
                    else:
                        for (s, ts_, co, n) in tok_ranges(job, tt_ * TN, TN):
                            base = s * (3 + Ls) + 3 + ts_
                            P.copy("act", xraw[:, base:base + n], pt[:, co:co + n])
                    yield
                if cidx is None:
                    continue
                for s in range(nseq):
                    base = s * (3 + Ls)
                    P.copy("pool", xraw[:, base:base + 3], ctail[:, l, ss_(s), :, cidx])
                    yv = ycv[:, s * Ls:(s + 1) * Ls]
                    P.ts("dve", yv, xraw[:, base + 3:base + 3 + Ls], pr(CW + 3 * 12 + cidx), pr(CB + cidx), ALU.mult, ALU.add)
                    yield
                    for tap in (2, 1, 0):
                        P.stt(yv, xraw[:, base + tap:base + tap + Ls], pr(CW + tap * 12 + cidx), yv, ALU.mult, ALU.add)
                        yield
                    P.copy("pool", ctail[:, l, ss_(s), :, cidx], xraw[:, base + Ls:base + Ls + 3])
                P.act(ycv[:, 0:T], ycv[:, 0:T], AF.Silu)
                yield
                if pi == 2:
                    P.copy("pool", vT_[:, 0:T], ycv[:, 0:T])
                    yield
                else:
                    P.act(sqb[:, 0:T], ycv[:, 0:T], AF.Square)
                    yield
                    for tt_ in range(NTT):
                        pt = bigr.get()[:, 0:TN]
                        P.mm(pt, ones_b[:], sqb[:, tt_ * TN:(tt_ + 1) * TN])
                        r = rt[tt_ % 2][:, 0:TN]
                        P.ts("dve", r, pt, 1.0, EPS, ALU.mult, ALU.add)
                        yield
                        P.act(r, r, AF.Ln)
                        yield
                        P.act(r, r, AF.Exp, scale=-0.5)
                        yield
                        if pi == 0:
                            P.stt(dst[:, tt_ * TN:(tt_ + 1) * TN], ycv[:, tt_ * TN:(tt_ + 1) * TN], 128 ** -0.5, r, ALU.mult, ALU.mult)
                        else:
                            P.tt("dve", dst[:, tt_ * TN:(tt_ + 1) * TN], ycv[:, tt_ * TN:(tt_ + 1) * TN], r, ALU.mult)
                        yield

        def pre_gen(h, i):
            qT_, kT_, vT_, zT_ = sets[h % 2]
            tl = slice(i * 128, (i + 1) * 128)
            sc = lambda c: SM[:, i, c:c + 1]
            tU, tWT, tqg, tQKm, tKt0, tKt1 = outs[i % 4]
            (tR, tD1, tD2, tdec, tdecT, tgam, tdecm, tdecTm) = ints[i % 2]
            (bA0, bA1, bB0, bB1, bX0, bX1, bKBG, bVB) = b16s[i % 2]
            P.ts("pool", tR, TRI[:], sc(8 + h), None, ALU.mult)
            pG = smr.get()
            P.mm(pG, ones_f[:], tR)
            P.ts("dve", tD1, pG, sc(20 + h), 0.0, ALU.subtract, ALU.max)
            P.ts("dve", tD2, pG, sc(20 + h), 0.0, ALU.subtract, ALU.min)
            yield
            P.act(tdec, tD1, AF.Exp, scale=-1.0)
            P.act(tdecT, tD2, AF.Exp)
            P.act(tgam, pG, AF.Exp)
            yield
            pkk = smr.get()
            P.mm(pkk, kT_[:, tl], kT_[:, tl])
            pqk = smr.get()
            P.mm(pqk, kT_[:, tl], qT_[:, tl])
            P.tt("pool", tdecm, tdec, MASKL[:], ALU.mult)
            P.tt("pool", tdecTm, tdecT, TRI[:], ALU.mult)
            yield
            P.stt(bA0, pkk, sc(16 + h), tdecm, ALU.mult, ALU.mult)
            P.tt("dve", tQKm, pqk, tdecTm, ALU.mult)
            P.tt("pool", tqg, qT_[:, tl], tgam, ALU.mult)
            yield
            pB = smr.get()
            P.mm(pB, bA0, ident_b[:])
            P.copy("act", bB0, pB)
            P.tt("dve", bX0, pB, ident_f[:], ALU.add)
            yield
            Ac, Bc, Xc = bA0, bB0, bX0
            An, Bn, Xn = bA1, bB1, bX1
            for k in range(1, 6):
                pA = smr.get()
                P.mm(pA, Bc, Ac)
                P.copy("act", An, pA)
                yield
                if k < 5:
                    pBn = smr.get()
                    P.mm(pBn, Ac, Bc)
                    P.copy("act", Bn, pBn)
                    yield
                pX = smr.get()
                P.mm(pX, An, Xc)
                P.tt("dve", Xn, pX, Xc, ALU.add)
                yield
                Ac, An = An, Ac
                Bc, Bn = Bn, Bc
                Xc, Xn = Xn, Xc
            pkt = smr.get()
            P.mm(pkt, kT_[:, tl], ident_b[:])
            P.ts("dve", bKBG, pkt, sc(36 + h), None, ALU.mult)
            P.ts("dve", tKt0, pkt, sc(24 + h), chkcol[:, 0:1], ALU.mult, ALU.mult)
            P.ts("dve", tKt1, pkt, sc(24 + h), chkcol[:, 1:2], ALU.mult, ALU.mult)
            yield
            pvt = smr.get()
            P.mm(pvt, vT_[:, tl], ident_b[:])
            P.ts("dve", bVB, pvt, sc(12 + h), None, ALU.mult)
            yield
            pU = smr.get()
            P.mm(pU, Xc, bVB)
            P.copy("act", tU, pU)
            pW = smr.get()
            P.mm(pW, bKBG, Xc)
            P.copy("act", tWT, pW)
            yield

        def scan_gen(h, i):
            sc = lambda c: SM[:, i, c:c + 1]
            tU, tWT, tqg, tQKm, tKt0, tKt1 = outs[i % 4]
            for jc in range(2):
                seq = (i * 128 + jc * 64) // Ls
                S = Sst[:, l, ss_(seq), h, :]
                cs = slice(jc * 64, (jc + 1) * 64)
                pWS = smr.get()
                P.mm(pWS, tWT, S)
                pO = smr.get()[:, 0:64]
                P.mm(pO, S, tqg[:, cs], start=True, stop=False)
                yield
                P.tt("dve", tdelta, tU, pWS, ALU.subtract)
                yield
                P.mm(pO, tdelta, tQKm[:, cs], start=False, stop=True)
                pSU = smr.get()
                P.mm(pSU, (tKt0, tKt1)[jc], tdelta)
                yield
                P.stt(S, S, sc(28 + 4 * jc + h), pSU, ALU.mult, ALU.add)
                P.copy("act", oT[:, i * 128 + jc * 64: i * 128 + (jc + 1) * 64], pO)
                yield

        def norm_gen(h):
            qT_, kT_, vT_, zT_ = sets[h % 2]
            sqb_ = sqb_n if NT == 1 else sqb
            rt_ = rt_n if NT == 1 else rt
            P.act(sqb_[:, 0:T], oT[:, 0:T], AF.Square)
            yield
            for tt_ in range(NTT):
                sl_ = slice(tt_ * TN, (tt_ + 1) * TN)
                pt = smr.get() if NT == 1 else bigr.get()[:, 0:TN]
                P.mm(pt, ones_b[:], sqb_[:, sl_])
                r = rt_[tt_ % 2][:, 0:TN]
                P.ts("dve", r, pt, 1.0 / 128, EPS, ALU.mult, ALU.add)
                yield
                P.act(r, r, AF.Ln)
                yield
                P.act(r, r, AF.Exp, scale=-0.5)
                yield
                P.stt(r, oT[:, sl_], pr(AN), r, ALU.mult, ALU.mult)
                yield
                P.tt("dve", mixT[:, h, sl_], r, zT_[:, sl_], ALU.mult)
                yield
            if last:
                for s in range(nseq):
                    P.store("act", OP("st")[l, bb(s), h], Sst[:, l, ss_(s), h, :], "st_out")

        def stageW(h):
            if NT == 1:
                yield from pre_gen(h, 0)
                yield from scan_gen(h, 0)
            else:
                yield from zipgen(pre_gen(h, 0), pre_gen(h, 1))
                for p_ in range(NT // 2):
                    gl_ = [chain(scan_gen(h, 2 * p_), scan_gen(h, 2 * p_ + 1))]
                    if 2 * p_ + 2 < NT:
                        gl_ += [pre_gen(h, 2 * p_ + 2), pre_gen(h, 2 * p_ + 3)]
                    yield from zipgen(*gl_)
            yield from norm_gen(h)

        run(zipgen(stageC(0), gates_gen()))
        ck(3)
        for h in range(4):
            gl2 = [stageW(h)]
            if h + 1 < 4:
                gl2.append(stageC(h + 1))
            if os.environ.get("KZIP", "1") == "0":
                for g_ in gl2:
                    run(g_)
            else:
                run(zipgen(*gl2))
        ck(4)
        if last:
            for s in range(nseq):
                store_tails(ctail[:, l, ss_(s), :, :].rearrange("p t c -> p (t c)"), 36,
                            OP("cv")[l, bb(s)].rearrange("t (c p) -> (t c) p", p=128), "tl_out")

        def out_proj_gen(wo, banks):
            rr_ = Ring([ps[b_][:] for b_ in banks])
            for i in range(NT):
                for half in range(2):
                    pt = rr_.get()
                    for kc in range(4):
                        P.mm(pt, mixT[:, kc, i * 128:(i + 1) * 128], wo[:, kc * 1024 + half * 512: kc * 1024 + half * 512 + 512],
                             start=(kc == 0), stop=(kc == 3))
                    P.tt("dve", x_tm[:, i, half * 512:(half + 1) * 512], x_tm[:, i, half * 512:(half + 1) * 512], pt, ALU.add)
                    yield

        wo0 = WS.get(("wo", l, 0))
        opg0 = out_proj_gen(wo0, [6, 7])

        o = 0
        QTall = [A16(o + 2 * T * hh, T) for hh in range(4)]; o += 8 * T
        PT = [[A16(o + 1024 * (2 * bfi + m), 512) for m in range(2)] for bfi in range(3)]; o += 6144
        sq2 = A16(o, 512); o += 1024
        rd = A32(o, 512); o += 2048
        o1 = A32(o, 512); o += 2048
        o2 = A32(o, 512); o += 2048
        ktok = [A32(o, 512), A32(o + 2048, 512)]; o += 4096
        vtok = [A32(o, 512), A32(o + 2048, 512)]; o += 4096
        sO = [A32(o, 512), A32(o + 2048, 512)]; o += 4096
        sD = [A32(o, 512), A32(o + 2048, 512)]; o += 4096
        assert o <= ARENA_B
        rr = Ring([ps[0][:], ps[1][:]])
        wbk = WS.get(("bk", l))
        wbv = WS.get(("bv", l), ahead=1)
        if job.kind == "p" and job.chunk == 0:
            KTnew = lambda h: KTc[:, (l * 4 + h) * 1024:(l * 4 + h) * 1024 + T]
            Vnew = lambda i, h: Vc[:, (l * 8 + i) * 512 + h * 128:(l * 8 + i) * 512 + (h + 1) * 128]
            Vnew_full = lambda i: Vc[:, (l * 8 + i) * 512:(l * 8 + i + 1) * 512]
        else:
            KTnew = lambda h: KTn[:, h * 1024:h * 1024 + T]
            Vnew = lambda i, h: Vn[:, i * 512 + h * 128:i * 512 + (h + 1) * 128]
            Vnew_full = lambda i: Vn[:, i * 512:(i + 1) * 512]
        def kv_gen():
            for i in range(NT):
                for wsel, tokb, oname in ((wbk, ktok, "k"), (wbv, vtok, "v")):
                    pt = rr.get()
                    for kc in range(8):
                        P.mm(pt, hT[:, kc, i * 128:(i + 1) * 128], wsel[:, kc * 512:(kc + 1) * 512], start=(kc == 0), stop=(kc == 7))
                    tb = tokb[i % 2]
                    P.copy("act", tb, pt)
                    if oname == "v":
                        P.copy("dve", Vnew_full(i), pt)
                    for (s, ts_, co, n) in tok_ranges(job, i * 128, 128):
                        t_abs = ts_ + (job.chunk * TCH if job.kind == "p" else 0)
                        P.store("act", OP(oname)[l, bb(s), t_abs:t_abs + n, :], tb[co:co + n, :], "%stok%d" % (oname, i % 2))
                    yield

        run(zipgen(opg0, kv_gen()))
        ck(5)
        ck(6)
        def load_cache(s):
            for kb in range(16):
                st_ = vtok[kb % 2]
                P.load("sp", st_, I["cache_v"][l, s, kb * 128:(kb + 1) * 128, :], "cv_l%d" % (kb % 2))
                P.copy("dve" if kb % 2 else "pool", Vc[:, kb * 512:(kb + 1) * 512], st_)
            for kb in range(16):
                st_ = ktok[kb % 2]
                P.load("sp", st_, I["cache_k"][l, s, kb * 128:(kb + 1) * 128, :], "ck_l%d" % (kb % 2))
                pt = rr.get()
                for h in range(4):
                    P.mm(pt[:, h * 128:(h + 1) * 128], st_[:, h * 128:(h + 1) * 128], ident_f[:])
                for h in range(4):
                    P.copy("act" if h % 2 else "dve", KTc[:, h * 2048 + kb * 128: h * 2048 + (kb + 1) * 128], pt[:, h * 128:(h + 1) * 128])

        accO = [ps[2][:], ps[3][:]]
        accD = [ps[4][:], ps[5][:]]
        sring = Ring([ps[0][:], ps[1][:], ps[6][:], ps[7][:]])
        for h in range(4):
            wB = WS.get(("B", l, h))
            for pi in range(2):
                dst = QTall[h] if pi == 0 else KTnew(h)
                for tt_ in range(NTT):
                    pt = sring.get()[:, 0:TN]
                    for kc in range(8):
                        P.mm(pt, wB[:, (pi * 8 + kc) * 128:(pi * 8 + kc + 1) * 128], hT[:, kc, tt_ * TN:(tt_ + 1) * TN],
                             start=(kc == 0), stop=(kc == 7))
                    P.copy("act", dst[:, tt_ * TN:(tt_ + 1) * TN], pt)
        def blocks_gen(s, h, qt):
            QT = QTall[h]
            ktn = KTnew(h)
            nq = min(512, Ls)
            q0 = s * Ls + qt * 512
            blocks = []
            if job.kind == "s":
                for kb in range(16):
                    blocks.append((KTc[:, h * 2048 + kb * 128: h * 2048 + (kb + 1) * 128],
                                   Vc[:, kb * 512 + h * 128: kb * 512 + (h + 1) * 128], 0, nq, False, None))
                blocks.append((ktn[:, 0:128], Vnew(0, h), 0, nq, False, s))
            else:
                if job.chunk == 1:
                    for kb in range(8):
                        blocks.append((KTc[:, (l * 4 + h) * 1024 + kb * 128:(l * 4 + h) * 1024 + (kb + 1) * 128],
                                       Vc[:, (l * 8 + kb) * 512 + h * 128:(l * 8 + kb) * 512 + (h + 1) * 128], 0, nq, False, None))
                for kb in range(4 * qt + 4):
                    qlo = max(qt * 512, kb * 128)
                    blocks.append((ktn[:, kb * 128:(kb + 1) * 128], Vnew(kb, h), qlo - qt * 512, qt * 512 + 512 - qlo,
                                   kb >= 4 * qt, None))
            nb = len(blocks)
            pend = None

            def flush(pend_):
                for (m_, pt__, v__, qoff_, n_, bi_) in pend_:
                    P.mm(accO[m_][:, qoff_:qoff_ + n_], v__, pt__, start=(bi_ == 0), stop=(bi_ == nb - 1))
                    P.mm(accD[m_][:, qoff_:qoff_ + n_], ones_b[:], pt__, start=(bi_ == 0), stop=(bi_ == nb - 1))

            for bi, (kt_ap, v_ap, qoff, n, diag, rowmask) in enumerate(blocks):
                cur = []
                for m in range(2):
                    pS = sring.get()[:, 0:n]
                    P.mm(pS, kt_ap[64 * m:64 * m + 64, :], QT[64 * m:64 * m + 64, q0 + qoff:q0 + qoff + n])
                    pt_ = PT[bi % 3][m][:, 0:n]
                    P.act(pt_, pS, AF.Exp, scale=0.125)
                    if diag:
                        P.memset("pool", pt_[64:128, 0:64], 0.0)
                    if rowmask is not None:
                        other = 1 - rowmask
                        P.memset("pool", pt_[64 * other:64 * other + 64, :], 0.0)
                    cur.append((m, pt_, v_ap, qoff, n, bi))
                if pend is not None:
                    flush(pend)
                pend = cur
                yield
            flush(pend)
            P.copy("act", sO[0][:, 0:nq], accO[0][:, 0:nq])
            P.copy("dve", sD[0][:, 0:nq], accD[0][:, 0:nq])
            yield
            P.copy("act", sO[1][:, 0:nq], accO[1][:, 0:nq])
            P.copy("dve", sD[1][:, 0:nq], accD[1][:, 0:nq])
            yield

        def combine_gen(s, h, qt):
            nq = min(512, Ls)
            q0 = s * Ls + qt * 512
            P.act(rd[:, 0:nq], sD[0][:, 0:nq], AF.Ln)
            yield
            P.act(rd[:, 0:nq], rd[:, 0:nq], AF.Exp, scale=-1.0)
            yield
            P.tt("dve", o1[:, 0:nq], sO[0][:, 0:nq], rd[:, 0:nq], ALU.mult)
            yield
            P.act(rd[:, 0:nq], sD[1][:, 0:nq], AF.Ln)
            yield
            P.act(rd[:, 0:nq], rd[:, 0:nq], AF.Exp, scale=-1.0)
            yield
            P.tt("dve", o2[:, 0:nq], sO[1][:, 0:nq], rd[:, 0:nq], ALU.mult)
            yield
            P.stt(o1[:, 0:nq], o2[:, 0:nq], bcast[:, l, NEGLAM:NEGLAM + 1], o1[:, 0:nq], ALU.mult, ALU.add)
            yield
            P.act(sq2[:, 0:nq], o1[:, 0:nq], AF.Square)
            yield
            pn = sring.get()[:, 0:nq]
            P.mm(pn, ones_b[:], sq2[:, 0:nq])
            P.ts("dve", rd[:, 0:nq], pn, 1.0 / 128, EPS, ALU.mult, ALU.add)
            yield
            P.act(rd[:, 0:nq], rd[:, 0:nq], AF.Ln)
            yield
            P.act(rd[:, 0:nq], rd[:, 0:nq], AF.Exp, scale=-0.5)
            yield
            P.stt(mixT[:, h, q0:q0 + nq], o1[:, 0:nq], bnS[:, l:l + 1], rd[:, 0:nq], ALU.mult, ALU.mult)
            yield

        def runB(g_):
            for _ in g_:
                pass

        def zipB(*gens):
            gens = list(gens)
            while gens:
                for g_ in list(gens):
                    try:
                        next(g_)
                    except StopIteration:
                        gens.remove(g_)

        for s in range(nseq):
            if job.kind == "s":
                load_cache(s)
            groups = [(s, h, qt) for h in range(4) for qt in range(max(1, Ls // 512))]
            if job.kind == "s":
                for gq in groups:
                    runB(blocks_gen(*gq))
                    runB(combine_gen(*gq))
            else:
                runB(blocks_gen(*groups[0]))
                for gi in range(len(groups)):
                    if gi + 1 < len(groups):
                        zipB(combine_gen(*groups[gi]), blocks_gen(*groups[gi + 1]))
                    else:
                        runB(combine_gen(*groups[gi]))
        wo1 = WS.get(("wo", l, 1))
        op1_state = {"done": False}

        def opg1_wrap():
            yield from out_proj_gen(wo1, [6, 7])
            op1_state["done"] = True

        opg1 = opg1_wrap()
        ck(7)

        o = 0
        XC = nseq * (15 + Ls)
        XCB = 4 * ((XC + 31) // 32 * 32)
        cbufs = []
        for _cb in range(2):
            xc_ = A32(o, XC); o += XCB
            pa_ = A32(o, XC); o += XCB
            pbb_ = A32(o, XC); o += XCB
            pgb_ = A16(o, T); o += 2 * T
            t16_ = A32(o, 16); o += 64
            cbufs.append((xc_, pa_, pbb_, pgb_, t16_))
        assert o <= ARENA_B, o
        wcx = WS.get(("cx", l))
        rr = Ring([ps[0][:], ps[1][:]])
        if job.kind == "s":
            for s in range(nseq):
                load_tails(ptail[:, l, s, :, :].rearrange("p t c -> p (t c)"), 60,
                           I["cache_pool"][l, s].rearrange("t (c p) -> (t c) p", p=128))

        def zipgenC(*gens):
            gens = list(gens)
            while gens:
                for g_ in list(gens):
                    try:
                        next(g_)
                    except StopIteration:
                        gens.remove(g_)

        def Cgen(g, win):
            xc, pa, pb_, pgb, t16 = cbufs[g % 2]
            for tt_ in range(NTT):
                pt = rr.get()[:, 0:TN]
                for kc in range(8):
                    P.mm(pt, wcx[:, kc * 512 + g * 128: kc * 512 + (g + 1) * 128], hT[:, kc, tt_ * TN:(tt_ + 1) * TN],
                         start=(kc == 0), stop=(kc == 7))
                yield
                for (s, ts_, co, n) in tok_ranges(job, tt_ * TN, TN):
                    base = s * (15 + Ls) + 15 + ts_
                    P.copy("act", xc[:, base:base + n], pt[:, co:co + n])
                yield
            for s in range(nseq):
                base = s * (15 + Ls)
                N = 15 + Ls
                if first:
                    P.memset("pool", xc[:, base:base + 15], 0.0)
                else:
                    P.copy("pool", xc[:, base:base + 15], ptail[:, l, ss_(s), :, g])
                e_ = lambda a, b_: xc[:, base + a:base + b_]
                A_ = lambda a, b_: pa[:, base + a:base + b_]
                B_ = lambda a, b_: pb_[:, base + a:base + b_]
                P.tt("pool", A_(1, N), e_(1, N), e_(0, N - 1), ALU.add)
                yield
                cur = A_
                if g >= 1:
                    P.tt("pool", B_(3, N), A_(3, N), A_(1, N - 2), ALU.add)
                    cur = B_
                    yield
                if g >= 2:
                    P.tt("pool", A_(7, N), B_(7, N), B_(3, N - 4), ALU.add)
                    cur = A_
                    yield
                if g >= 3:
                    P.tt("pool", B_(15, N), A_(15, N), A_(7, N - 8), ALU.add)
                    cur = B_
                    yield
                P.stt(pgb[:, s * Ls:(s + 1) * Ls], cur(15, N), 1.0 / win, e_(15, N), ALU.mult, ALU.subtract)
                if first:
                    P.tt("dve", t16, cur(15, 31), invcnt[:, g, :], ALU.mult)
                    P.tt("dve", pgb[:, s * Ls:s * Ls + 16], t16, e_(15, 31), ALU.subtract)
                P.copy("pool", ptail[:, l, ss_(s), :, g], e_(Ls, Ls + 15))
                yield
            while not op1_state["done"]:
                yield
            for tt_ in range(NTT):
                pt = rr.get()[:, 0:TN]
                P.mm(pt, cwb[:, l, g, :], pgb[:, tt_ * TN:(tt_ + 1) * TN])
                P.act(mixT[:, g, tt_ * TN:(tt_ + 1) * TN], pt, AF.Identity, scale=pr(CS + g))
                yield

        wins = (2, 4, 8, 16)
        run(zipgen(opg1, chain(zipgen(Cgen(0, wins[0]), Cgen(1, wins[1])), zipgen(Cgen(2, wins[2]), Cgen(3, wins[3])))))
        if last:
            for s in range(nseq):
                store_tails(ptail[:, l, ss_(s), :, :].rearrange("p t c -> p (t c)"), 60,
                            OP("pl")[l, bb(s)].rearrange("t (c p) -> (t c) p", p=128), "tl_out")
        run(out_proj_gen(WS.get(("wo", l, 2)), [0, 1]))
        ck(8)

        if not _cast_state["l1_done"]:
            cast_layer(1)
            _cast_state["l1_done"] = True
        o = 0
        actb = lambda j, n: A16(j * 1024, n)
        o = 22 * 1024
        UW = (TN // Ls if Ls < TN else 1) * 2 + TN
        nsq_t = max(1, TN // Ls) if Ls < TN else 1
        fbufs = []
        for _pb in range(2):
            ug_ = A32(o, UW); o += 4 * UW
            uv_ = A32(o, UW); o += 4 * UW
            yg_ = A32(o, TN); o += 4 * TN
            yvv_ = A32(o, TN); o += 4 * TN
            fbufs.append((ug_, uv_, yg_, yvv_))
        assert o <= ARENA_B, o
        for s in range(nseq):
            if first:
                P.memset("pool", ftail[:, l, ss_(s), :, :], 0.0)
            elif job.kind == "s":
                load_tails(ftail[:, l, s, :, :].rearrange("p t c -> p (t c)"), 88,
                           I["cache_ffn_conv"][l, s].rearrange("t (c p) -> (t c) p", p=128))
        for ft in range(NTT):
            nsub = TN // 128
            rms_to_hT(lambda i: x_tm[:, ft * nsub + i, :], nsub, GF, l)
            rr = Ring([ps[0][:], ps[1][:], ps[2][:], ps[3][:]])
            ranges = tok_ranges(job, ft * TN, TN)
            for jg in range(6):
                nj = 4 if jg < 5 else 2
                wg = WS.get(("ug", l, ft, jg))
                wv = WS.get(("uv", l, ft, jg))
                for jj in range(nj):
                    j = jg * 4 + jj
                    ug, uv, yg, yv_ = fbufs[j % 2]
                    for (wsel, ub, yb, cix) in ((wg, ug, yg, j), (wv, uv, yv_, 22 + j)):
                        pt = rr.get()[:, 0:TN]
                        for kc in range(8):
                            P.mm(pt, wsel[:, kc * nj * 128 + jj * 128: kc * nj * 128 + (jj + 1) * 128], hT[:, kc, 0:TN],
                                 start=(kc == 0), stop=(kc == 7))
                        for ri, (s, ts_, co, n) in enumerate(ranges):
                            ubase = ri * (2 + n)
                            P.copy("pool", ub[:, ubase:ubase + 2], ftail[:, l, ss_(s), :, cix])
                            P.copy("act", ub[:, ubase + 2:ubase + 2 + n], pt[:, co:co + n])
                            yy = yb[:, co:co + n]
                            P.ts("dve", yy, ub[:, ubase + 2:ubase + 2 + n], pr(FW + 2 * 44 + cix), pr(FB + cix), ALU.mult, ALU.add)
                            P.stt(yy, ub[:, ubase + 1:ubase + 1 + n], pr(FW + 44 + cix), yy, ALU.mult, ALU.add)
                            P.stt(yy, ub[:, ubase:ubase + n], pr(FW + cix), yy, ALU.mult, ALU.add)
                            P.copy("pool", ftail[:, l, ss_(s), :, cix], ub[:, ubase + n:ubase + n + 2])
                    P.act(yg[:, 0:TN], yg[:, 0:TN], AF.Silu)
                    P.tt("pool", actb(j, TN), yg[:, 0:TN], yv_[:, 0:TN], ALU.mult)
            for jg in range(6):
                nj = 4 if jg < 5 else 2
                wd = WS.get(("dn", l, ft, jg))
                for jj in range(nj):
                    j = jg * 4 + jj
                    for i in range(nsub):
                        for half in range(2):
                            P.mm(ps[i * 2 + half][:], actb(j, TN)[:, i * 128:(i + 1) * 128],
                                 wd[:, jj * 1024 + half * 512: jj * 1024 + (half + 1) * 512], start=(j == 0), stop=(j == 21))
            for i in range(nsub):
                for half in range(2):
                    xs_ = x_tm[:, ft * nsub + i, half * 512:(half + 1) * 512]
                    P.tt("dve", xs_, xs_, ps[i * 2 + half][:], ALU.add)
        if last:
            for s in range(nseq):
                store_tails(ftail[:, l, ss_(s), :, :].rearrange("p t c -> p (t c)"), 88,
                            OP("ff")[l, bb(s)].rearrange("t (c p) -> (t c) p", p=128), "tl_out")
        ck(9)

    def final_norm(job):
        NT = job.T // 128
        P.memset("pool", nrm_s[:, 0:NT], 0.0)
        for i in range(NT):
            P.act(junk[:], x_tm[:, i, :], AF.Square, accum_out=nrm_s[:, i:i + 1])
        P.ts("dve", nrm_s[:, 8:8 + NT], nrm_s[:, 0:NT], 1.0 / D, EPS, ALU.mult, ALU.add)
        P.act(nrm_s[:, 8:8 + NT], nrm_s[:, 8:8 + NT], AF.Sqrt)
        P.recip(nrm_s[:, 8:8 + NT], nrm_s[:, 8:8 + NT])
        for i in range(NT):
            yb = (A32(0, D), A32(4096, D))[i % 2]
            P.stt(yb, x_tm[:, i, :], nrm_s[:, 8 + i:9 + i], gfin[:], ALU.mult, ALU.mult)
            for (s, ts_, co, n) in tok_ranges(job, i * 128, 128):
                if job.kind == "p":
                    t_abs = job.chunk * TCH + ts_
                    P.store("act", O["y_p"][job.b, t_abs:t_abs + n, :], yb[co:co + n, :], "y%d" % (i % 2))
                else:
                    P.store("act", O["y_s"][s, ts_:ts_ + n, :], yb[co:co + n, :], "y%d" % (i % 2))

    ck_jobs = jobs if STOP >= 99 else ([jobs[int(os.environ.get('KJOB', '0'))]] if STOP < 50 else jobs[:STOP - 50])
    if os.environ.get('KPOISON', '0') != '0':
        pv = float('nan') if os.environ['KPOISON'] == 'nan' else float(os.environ['KPOISON'])
        which = os.environ.get('KPOISON_WHICH', 'all').split(',')
        cand = {"x_tm": x_tm[:, :, :], "hT": hT[:, :, :], "mixT": mixT[:, :, :], "KTc": KTc[:, :], "Vc": Vc[:, :], "KTn": KTn[:, :],
                "Vn": Vn[:, :], "arena": ar32[:, :], "qk2": qk2[:, :], "junk": junk[:, :], "xsb": xsb[:, :], "smallA": smallA[:, :, :],
                "Sst": Sst[:, :, :, :, :], "ctail": ctail[:, :, :, :, :], "ptail": ptail[:, :, :, :, :], "ftail": ftail[:, :, :, :, :],
                "nrm_s": nrm_s[:, :], "tstg": tstg[:, :], "wsl0": wsl[0][:, :], "wsl1": wsl[1][:, :], "wsl2": wsl[2][:, :], "wsl3": wsl[3][:, :]}
        for nm, ap_ in cand.items():
            if 'all' in which or nm in which:
                P.memset("pool", ap_, pv)
        if 'all' in which or 'psum' in which:
            for b_ in range(8):
                P.memset("dve", ps[b_][:, :], pv)
    try:
        ck(0)
        for job in ck_jobs:
            NT = job.T // 128
            for i in range(NT):
                if job.kind == "p":
                    t0 = job.chunk * TCH + i * 128
                    P.load("sp", x_tm[:, i, :], I["x_prompt"][job.b, t0:t0 + 128, :], "x")
                else:
                    P.load("sp", x_tm[0:64, 0, :], I["x_sample"][0], "x")
                    P.load("sp", x_tm[64:128, 0, :], I["x_sample"][1], "x")
            for l in range(2):
                layer_step(job, l)
            final_norm(job)
    except _Stop:
        if os.environ.get('KDUMP', '0') == '1':
            for i in range(8):
                P.store("act", O["y_p"][0, i * 128:(i + 1) * 128, :], x_tm[:, i, :], "ydump")

    keys = [k for k in P.dma_issued if k.startswith(("y", "st_out", "tl_out", "ktok", "vtok"))]
    P.emit(final_wait_keys=keys)
    es.close()
    return nc, P


_CACHE = {}


def kernel(**inputs):
    if "nc" not in _CACHE:
        _CACHE["nc"] = build_program()[0]
    nc = _CACHE["nc"]
    n = 8
    f = lambda a: np.ascontiguousarray(np.asarray(a, dtype=np.float32))
    in_maps = []
    for c in range(n):
        bs = slice(2 * c, 2 * c + 2)
        m = {}
        m["x_prompt"] = f(inputs["x_prompt"][bs])
        m["x_sample"] = f(inputs["x_sample"][bs])
        m["state_delta"] = f(inputs["state_delta"][:, bs])
        m["cache_qkv_conv"] = f(inputs["cache_qkv_conv"][:, bs])
        m["cache_k"] = f(inputs["cache_k"][:, bs]).reshape(2, 2, 2048, 512)
        m["cache_v"] = f(inputs["cache_v"][:, bs]).reshape(2, 2, 2048, 512)
        m["cache_pool"] = f(inputs["cache_pool"][:, bs])
        m["cache_ffn_conv"] = f(inputs["cache_ffn_conv"][:, bs])
        for k in ("norm_mix", "w_in", "a_conv_w", "a_conv_b", "a_log", "a_dt_bias", "a_norm", "b_norm", "c_w",
                  "c_scale", "w_out", "norm_ffn", "ffn_up", "ffn_conv_w", "ffn_conv_b", "ffn_down", "norm_final"):
            m[k] = f(inputs[k])
        m["b_lambda"] = f(inputs["b_lambda"]).reshape(2, 256)
        in_maps.append(m)
    res = run_bass_kernel_spmd(nc, in_maps, core_ids=list(range(n)))
    R = res.results
    cat0 = lambda k: np.concatenate([r[k] for r in R], axis=0)
    cat1 = lambda k: np.concatenate([r[k] for r in R], axis=1)
    outs = {
        "y_p": cat0("y_p"), "y_s": cat0("y_s"),
        "st_p": cat1("st_p"), "st_s": cat1("st_s"),
        "cv_p": cat1("cv_p"), "cv_s": cat1("cv_s"),
        "k_p": cat1("k_p").reshape(2, 16, 2048, 4, 128), "k_s": cat1("k_s").reshape(2, 16, 64, 4, 128),
        "v_p": cat1("v_p").reshape(2, 16, 2048, 4, 128), "v_s": cat1("v_s").reshape(2, 16, 64, 4, 128),
        "pl_p": cat1("pl_p"), "pl_s": cat1("pl_s"),
        "ff_p": cat1("ff_p"), "ff_s": cat1("ff_s"),
    }
    return tuple(np.ascontiguousarray(outs[k], dtype=np.float32) for k in OUT_ORDER)
```
